# Optimizing a Trainium2 kernel written in Bass

```python
import jax
import jax.numpy as jnp
from jax import lax
import numpy as np

D_MODEL = 1024
BATCH = 8
SEQ = 4096
DEPTH = 1

EPS = 1e-6
N_MEM = 256
Q_BLOCK = 128

A_HEADS = 8
A_HEAD_DIM = 64
A_WIDTH = A_HEADS * A_HEAD_DIM
DILATED_BRANCHES = ((128, 1), (512, 4), (2048, 16))

B_HEADS = 4
B_NOPE = 128
B_ROPE = 64
B_V = 128
B_WIDTH = B_HEADS * B_V
Q_LORA = 384
KV_LORA = 256
ROPE_THETA = 10000.0

MIX_WIDTH = A_WIDTH + B_WIDTH
IN_SPLITS = (A_WIDTH, A_WIDTH, A_WIDTH, Q_LORA, KV_LORA, B_ROPE)
D_IN = sum(IN_SPLITS)
SPLIT_POINTS = tuple(sum(IN_SPLITS[:i + 1]) for i in range(len(IN_SPLITS) - 1))

M_HEADS = 4
M_HEAD_DIM = 128
M_WIDTH = M_HEADS * M_HEAD_DIM

D_FF = -(-8 * D_MODEL // (3 * 256)) * 256

kernel_name = 'hymba_dilated_mla_memory_encoder'


def rms_norm(x, g):
    xf = x.astype(jnp.float32)
    y = xf * lax.rsqrt(jnp.mean(xf * xf, axis=-1, keepdims=True) + EPS)
    return (y * g.astype(jnp.float32)).astype(x.dtype)


def alibi_slopes(n):
    return 2.0 ** (-8.0 * jnp.arange(1, n + 1, dtype=jnp.float32) / n)


def apply_rope(t, cos, sin):
    tf = t.astype(jnp.float32)
    t1, t2 = jnp.split(tf, 2, axis=-1)
    return jnp.concatenate([t1 * cos - t2 * sin, t2 * cos + t1 * sin], axis=-1).astype(t.dtype)


def dilated_attention(q, k, v, positions):
    S = q.shape[1]
    scale = A_HEAD_DIM ** -0.5
    slopes = alibi_slopes(A_HEADS)

    def block(start):
        t = start + jnp.arange(Q_BLOCK)
        qb = lax.dynamic_slice_in_dim(q, start, Q_BLOCK, axis=1)
        pq = lax.dynamic_slice_in_dim(positions, start, Q_BLOCK, axis=1)
        outs, lses = [], []
        for window, dil in DILATED_BRANCHES:
            n = window // (2 * dil)
            offs = jnp.arange(-n, n + 1) * dil
            idx = t[:, None] + offs[None, :]
            valid = (idx >= 0) & (idx < S)
            idx = jnp.clip(idx, 0, S - 1)
            kg = k[:, idx]
            vg = v[:, idx]
            pk = positions[:, idx]
            dist = jnp.abs(pq[:, :, None] - pk).astype(jnp.float32)
            s = jnp.einsum('bqhd,bqkhd->bhqk', qb, kg).astype(jnp.float32) * scale
            s = s - slopes[None, :, None, None] * dist[:, None]
            s = jnp.where(valid[None, None], s, -jnp.inf)
            lse = jax.nn.logsumexp(s, axis=-1)
            p = jnp.exp(s - lse[..., None]).astype(v.dtype)
            outs.append(jnp.einsum('bhqk,bqkhd->bqhd', p, vg))
            lses.append(lse)
        alpha = jax.nn.softmax(jnp.stack(lses), axis=0)
        alpha = jnp.transpose(alpha, (0, 1, 3, 2))[..., None]
        o = jnp.sum(alpha * jnp.stack(outs).astype(jnp.float32), axis=0)
        return o.astype(q.dtype)

    starts = jnp.arange(S // Q_BLOCK) * Q_BLOCK
    o = lax.map(block, starts)
    return jnp.moveaxis(o, 0, 1).reshape(q.shape)


def dense_attention(q, k, v, scale):
    S = q.shape[1]

    def block(start):
        qb = lax.dynamic_slice_in_dim(q, start, Q_BLOCK, axis=1)
        s = jnp.einsum('bqhd,bkhd->bhqk', qb, k).astype(jnp.float32) * scale
        p = jax.nn.softmax(s, axis=-1).astype(v.dtype)
        return jnp.einsum('bhqk,bkhd->bqhd', p, v)

    starts = jnp.arange(S // Q_BLOCK) * Q_BLOCK
    o = lax.map(block, starts)
    return jnp.moveaxis(o, 0, 1).reshape(q.shape[:3] + (v.shape[-1],))


def setup_inputs(seed: int = 0) -> dict:
    key = jax.random.key(seed)
    ks = jax.random.split(key, 24)

    def dense(k, fan_in, fan_out):
        return jax.random.normal(k, (DEPTH, fan_in, fan_out), jnp.float32) * fan_in ** -0.5

    def gain(k, n):
        return 1.0 + 0.02 * jax.random.normal(k, (DEPTH, n), jnp.float32)

    x = jax.random.normal(ks[0], (BATCH, SEQ, D_MODEL), jnp.float32)
    mem = jax.random.normal(ks[1], (BATCH, N_MEM, D_MODEL), jnp.float32)
    offset = jax.random.randint(ks[2], (BATCH, 1), 0, 1024, dtype=jnp.int32)
    positions = jnp.arange(SEQ, dtype=jnp.int32)[None, :] + offset
    return {
        'x': x,
        'mem': mem,
        'positions': positions,
        'norm_mix': gain(ks[3], D_MODEL),
        'w_in': dense(ks[4], D_MODEL, D_IN),
        'q_norm': gain(ks[5], Q_LORA),
        'w_q_up': dense(ks[6], Q_LORA, B_HEADS * (B_NOPE + B_ROPE)),
        'kv_norm': gain(ks[7], KV_LORA),
        'w_kv_up': dense(ks[8], KV_LORA, B_HEADS * (B_NOPE + B_V)),
        'gout_a': gain(ks[9], A_WIDTH),
        'gout_b': gain(ks[10], B_WIDTH),
        'w_out': dense(ks[11], MIX_WIDTH, D_MODEL),
        'norm_mem_q': gain(ks[12], D_MODEL),
        'norm_mem_kv': gain(ks[13], D_MODEL),
        'w_mq': dense(ks[14], D_MODEL, M_WIDTH),
        'w_mkv': dense(ks[15], D_MODEL, 2 * M_WIDTH),
        'w_mo': dense(ks[16], M_WIDTH, D_MODEL),
        'norm_ffn': gain(ks[17], D_MODEL),
        'w_gate': dense(ks[18], D_MODEL, D_FF),
        'w_up': dense(ks[19], D_MODEL, D_FF),
        'w_down': dense(ks[20], D_FF, D_MODEL),
        'norm_final': 1.0 + 0.02 * jax.random.normal(ks[21], (D_MODEL,), jnp.float32),
    }


def reference(x, mem, positions, norm_mix, w_in, q_norm, w_q_up, kv_norm, w_kv_up,
              gout_a, gout_b, w_out, norm_mem_q, norm_mem_kv, w_mq, w_mkv, w_mo,
              norm_ffn, w_gate, w_up, w_down, norm_final):
    B, S = x.shape[0], x.shape[1]
    M = mem.shape[1]
    half = B_ROPE // 2
    inv_freq = ROPE_THETA ** (-jnp.arange(half, dtype=jnp.float32) / half)
    ang = positions.astype(jnp.float32)[..., None] * inv_freq
    cos, sin = jnp.cos(ang), jnp.sin(ang)

    for l in range(DEPTH):
        h = rms_norm(x, norm_mix[l])
        proj = h @ w_in[l]
        qa, ka, va, cq, ckv, kr = jnp.split(proj, SPLIT_POINTS, axis=-1)

        hs_a = (B, S, A_HEADS, A_HEAD_DIM)
        o_a = dilated_attention(qa.reshape(hs_a), ka.reshape(hs_a), va.reshape(hs_a), positions)
        o_a = o_a.reshape(B, S, A_WIDTH)

        qb = (rms_norm(cq, q_norm[l]) @ w_q_up[l]).reshape(B, S, B_HEADS, B_NOPE + B_ROPE)
        q_nope, q_pe = jnp.split(qb, [B_NOPE], axis=-1)
        q_pe = apply_rope(q_pe, cos[:, :, None, :], sin[:, :, None, :])
        kvb = (rms_norm(ckv, kv_norm[l]) @ w_kv_up[l]).reshape(B, S, B_HEADS, B_NOPE + B_V)
        k_nope, v_b = jnp.split(kvb, [B_NOPE], axis=-1)
        k_pe = apply_rope(kr, cos, sin)
        k_pe = jnp.broadcast_to(k_pe[:, :, None, :], (B, S, B_HEADS, B_ROPE))
        q_b = jnp.concatenate([q_nope, q_pe], axis=-1)
        k_b = jnp.concatenate([k_nope, k_pe], axis=-1)
        o_b = dense_attention(q_b, k_b, v_b, (B_NOPE + B_ROPE) ** -0.5).reshape(B, S, B_WIDTH)

        mixed = jnp.concatenate([rms_norm(o_a, gout_a[l]), rms_norm(o_b, gout_b[l])], axis=-1)
        x = x + mixed @ w_out[l]

        hq = rms_norm(x, norm_mem_q[l])
        mk = rms_norm(mem, norm_mem_kv[l])
        mq = (hq @ w_mq[l]).reshape(B, S, M_HEADS, M_HEAD_DIM)
        mkv = (mk @ w_mkv[l]).reshape(B, M, 2, M_HEADS, M_HEAD_DIM)
        mkk, mvv = mkv[:, :, 0], mkv[:, :, 1]
        s = jnp.einsum('bshd,bmhd->bhsm', mq, mkk).astype(jnp.float32) * M_HEAD_DIM ** -0.5
        p = jax.nn.softmax(s, axis=-1).astype(mvv.dtype)
        mo = jnp.einsum('bhsm,bmhd->bshd', p, mvv).reshape(B, S, M_WIDTH)
        x = x + mo @ w_mo[l]

        hf = rms_norm(x, norm_ffn[l])
        x = x + (jax.nn.silu(hf @ w_gate[l]) * (hf @ w_up[l])) @ w_down[l]

    return rms_norm(x, norm_final)
```

```python
import numpy as np
import ml_dtypes
import concourse.bass as bass
import concourse.mybir as mybir
from concourse.bass_utils import run_bass_kernel_spmd

F32 = mybir.dt.float32
BF16 = mybir.dt.bfloat16
I32 = mybir.dt.int32
U8 = mybir.dt.uint8
ALU = mybir.AluOpType
AF = mybir.ActivationFunctionType

S = 4096
D = 1024
NCORES = 8
EPS = 1e-6
NMEM = 256
DFF = 2816
ENGS = ('pe', 'act', 'dve', 'pool', 'sp')
DTSIZE = {F32: 4, BF16: 2, I32: 4, U8: 1}


class Buf:
    __slots__ = ('name', 'w', 'r', 'rd', 'sem', 'tot')

    def __init__(self, name=''):
        self.name = name
        self.w = None
        self.r = {}
        self.rd = []
        self.sem = None
        self.tot = 0


class Op:
    __slots__ = ('eng', 'fn', 'waits', 'inc', 'idx', 'semval', 'snap')


class Prog:
    def __init__(self, nc):
        self.nc = nc
        self.ops = {e: [] for e in ENGS}
        self.seen = {e: {e2: -1 for e2 in ENGS} for e in ENGS}
        self.seen_dma = {e: {} for e in ENGS}
        self.esem = {e: nc.alloc_semaphore('es_' + e) for e in ENGS}
        self.last_real = {e: None for e in ENGS}
        self.pending = {}
        self.nsem = 0

    def _add(self, eng, fn, deps, real=True):
        o = Op()
        o.eng = eng
        o.fn = fn
        o.inc = False
        o.idx = len(self.ops[eng])
        o.waits = []
        seen = self.seen[eng]
        for tok in deps:
            if tok is None:
                continue
            if tok[0] == 'op':
                d = tok[1]
                if d.eng == eng and eng == 'pe':
                    continue
                if seen[d.eng] >= d.idx:
                    continue
                seen[d.eng] = d.idx
                for e3, v in d.snap.items():
                    if seen[e3] < v:
                        seen[e3] = v
                d.inc = True
                o.waits.append(('op', d))
            else:
                _, buf, val = tok
                val = max(val, buf.tot)
                sd = self.seen_dma[eng]
                if sd.get(id(buf), 0) >= val:
                    continue
                sd[id(buf)] = val
                o.waits.append(('dma', buf, val))
        o.snap = dict(seen)
        self.ops[eng].append(o)
        if fn is not None and real:
            self.last_real[eng] = o
        return o

    @staticmethod
    def _deps(reads, writes):
        deps = []
        for b in reads:
            deps.append(b.w)
        for b in writes:
            deps.append(b.w)
            deps.extend(b.r.values())
            deps.extend(b.rd)
        return deps

    def op(self, eng, fn, reads=(), writes=()):
        o = self._add(eng, fn, self._deps(reads, writes))
        tok = ('op', o)
        for b in reads:
            b.r[eng] = tok
        for b in writes:
            b.w = tok
            b.r = {}
            b.rd = []
        return o

    def dma(self, q, out_ap, in_ap, reads=(), writes=(), sembuf=None, **kw):
        deps = self._deps(reads, writes)
        sb = sembuf if sembuf is not None else (writes[0] if writes else reads[0])
        if sb.sem is None:
            sb.sem = self.nc.alloc_semaphore('ds%d' % self.nsem)
            self.nsem += 1
        o = self._add(q, None, deps, real=False)
        sb.tot += 16
        val = sb.tot
        sem = sb.sem
        o.fn = lambda e: e.dma_start(out=out_ap, in_=in_ap, **kw).then_inc(sem, 16)
        tok = ('dma', sb, val)
        for b in reads:
            b.rd.append(tok)
        for b in writes:
            b.w = tok
            b.r = {}
            b.rd = []
        self.pending[id(sb)] = sb
        return tok

    def barrier(self):
        deps = [('dma', sb, sb.tot) for sb in self.pending.values()]
        self._add('sp', lambda e: e.nop(), deps)
        lasts = {e: self.last_real[e] for e in ENGS if self.last_real[e] is not None}
        for e in ENGS:
            self._add(e, None, [('op', o) for e2, o in lasts.items()])

    def finish(self):
        deps = [('dma', sb, sb.tot) for sb in self.pending.values()]
        self._add('sp', lambda e: e.nop(), deps)

    def emit(self):
        nc = self.nc
        for e in ENGS:
            c = 0
            for o in self.ops[e]:
                if o.inc:
                    c += 1
                    o.semval = c
        esem = self.esem

        def mk(ename):
            def body(eng):
                for o in self.ops[ename]:
                    for w in o.waits:
                        if w[0] == 'op':
                            eng.wait_ge(esem[w[1].eng], w[1].semval)
                        else:
                            eng.wait_ge(w[1].sem, w[2])
                    if o.fn is not None:
                        ins = o.fn(eng)
                        if o.inc:
                            ins.then_inc(esem[ename], 1)
            return body

        with nc.Block() as block:
            block.sync(mk('sp'))
            block.scalar(mk('act'))
            block.vector(mk('dve'))
            block.gpsimd(mk('pool'))
            block.tensor(mk('pe'))


class Arena:
    def __init__(self, nc, nbytes):
        self.nbytes = nbytes
        self.t = nc.alloc_sbuf_tensor("arena", [128, nbytes], U8)

    def view(self, off, shape, dtype, p0=0):
        n = 1
        for s_ in shape[1:]:
            n *= s_
        nb = n * DTSIZE[dtype]
        assert off % 4 == 0 and off + nb <= self.nbytes, (off, nb, self.nbytes)
        ap = self.t[p0:p0 + shape[0], off:off + nb].bitcast(dtype)
        if len(shape) == 3:
            ap = ap.rearrange("p (a b) -> p a b", a=shape[1])
        elif len(shape) == 4:
            ap = ap.rearrange("p (a b c) -> p a b c", a=shape[1], b=shape[2])
        return ap


def layout_tokens(L, j):
    if L == 0:
        return j * 128, 1
    if L == 1:
        r, i0 = j // 8, (j % 8) * 128
        return r + 4 * i0, 4
    r, i0 = j // 2, (j % 2) * 128
    return r + 16 * i0, 16


def nat_slice(L, r, lo, hi):
    d = (1, 4, 16)[L]
    return slice(r + d * lo, r + d * (hi - 1) + 1, d)


A_SLOPES = [2.0 ** (-(i + 1)) for i in range(8)]


def build(stage=99):
    nc = bass.Bass("TRN2", target_bir_lowering=False)
    P = Prog(nc)

    def din(name, shape, dt=F32):
        return nc.dram_tensor(name, list(shape), dt, kind="ExternalInput").ap()

    x = din("x", [S, D])
    mem = din("mem", [NMEM, D])
    pos = din("pos", [1, S], I32)
    posk_in = din("posk", [128, 96], I32)
    cst = din("cst", [128, 1024])
    gcols = din("gcols", [128, 16])
    gcols2 = din("gcols2", [128, 16])
    g_mix = din("g_mix", [1, D])
    g_memq = din("g_memq", [1, D])
    g_memkv = din("g_memkv", [1, D])
    g_ffn = din("g_ffn", [1, D])
    g_fin = din("g_fin", [1, D])
    w_in = din("w_in", [128, 8, 2432])
    w_qup = din("w_qup", [128, 3, 1536])
    w_kvup = din("w_kvup", [128, 2, 1024])
    w_out = din("w_out", [128, 8, 1024])
    w_mq = din("w_mq", [128, 8, 512])
    w_mkv = din("w_mkv", [128, 8, 1024])
    w_mo = din("w_mo", [128, 4, 1024])
    w_gate = din("w_gate", [128, 22, 8, 128])
    w_up = din("w_up", [128, 22, 8, 128])
    w_down = din("w_down", [128, 22, D])
    out = nc.dram_tensor("out", [S, D], F32, kind="ExternalOutput").ap()
    dbg = nc.dram_tensor("dbg", [128, 8, S], F32, kind="ExternalOutput").ap() if stage != 99 else None

    wg_bf = nc.dram_tensor("wg_bf", [128, 22, 8, 128], BF16).ap()
    wu_bf = nc.dram_tensor("wu_bf", [128, 22, 8, 128], BF16).ap()
    wd_bf = nc.dram_tensor("wd_bf", [128, 22, D], BF16).ap()
    wo_bf = nc.dram_tensor("wo_bf", [128, 8, D], BF16).ap()
    B_wgbf, B_wubf, B_wdbf, B_wobf = Buf("wgbf"), Buf("wubf"), Buf("wdbf"), Buf("wobf")
    AR = Arena(nc, 206 * 1024)
    ps = [nc.alloc_psum_tensor("ps%d" % i, [128, 512], F32) for i in range(8)]
    psb = [Buf("ps%d" % i) for i in range(8)]

    ident = AR.view(0, [128, 128], BF16)
    ones = AR.view(256, [128, 128], BF16)
    zero = AR.view(512, [128, 512], BF16)
    negsl = AR.view(1536, [128, 8, 128], BF16)
    maskb = AR.view(3584, [128, 256], BF16)
    Emat = AR.view(4096, [64, 2, 128], BF16)
    sel = AR.view(4608, [65, 64], F32)
    gc = AR.view(4864, [128, 16], F32)
    poskf = AR.view(4928, [128, 96], F32)
    small = AR.view(5312, [128, 64], F32)
    negposk = AR.view(7616, [128, 96], F32)
    gc2 = AR.view(8000, [128, 16], F32)
    B_const = Buf("const")
    B_small = [Buf("small%d" % i) for i in range(64)]
    posq = AR.view(8192, [128, S], F32)
    B_posq = Buf("posq")
    OAT = 24576
    OBT_P2 = 57344
    HT = 90112
    LOC = 155648
    oaT = AR.view(OAT, [128, 4, S], BF16)
    B_oaT = [[Buf("oaT%d_%d" % (c, q)) for q in range(8)] for c in range(4)]
    hT = AR.view(HT, [128, 8, S], BF16)
    B_hT = [Buf("hT%d" % q) for q in range(8)]

    cst_s = AR.view(LOC, [128, 1024], F32)
    posi = AR.view(LOC + 4096, [128, S], I32)
    poski = AR.view(LOC + 4096 + 16384, [128, 96], I32)
    B_cst = Buf("cst_s")
    B_posi = Buf("posi")
    B_poski = Buf("poski")
    P.dma('sp', cst_s, cst, writes=[B_cst])
    P.dma('sp', posi, pos.partition_broadcast(128)[:, 0, :] if False else bass.AP(pos.tensor, 0, [[0, 128], [1, S]]),
          writes=[B_posi])
    P.dma('sp', poski, posk_in, writes=[B_poski])
    B_gc = Buf("gc")
    P.dma('sp', gc, gcols, writes=[B_gc])
    P.dma('sp', gc2, gcols2, writes=[B_gc])
    P.op('dve', lambda e: e.tensor_copy(out=ident, in_=cst_s[:, 0:128]), reads=[B_cst], writes=[B_const])
    P.op('dve', lambda e: e.tensor_copy(out=maskb, in_=cst_s[:, 128:384]), reads=[B_cst], writes=[B_const])
    P.op('dve', lambda e: e.tensor_copy(out=Emat[:, 0, :], in_=cst_s[0:64, 384:512]), reads=[B_cst], writes=[B_const])
    P.op('dve', lambda e: e.tensor_copy(out=Emat[:, 1, :], in_=cst_s[0:64, 512:640]), reads=[B_cst], writes=[B_const])
    P.op('dve', lambda e: e.tensor_copy(out=sel, in_=cst_s[0:65, 640:704]), reads=[B_cst], writes=[B_const])
    P.op('dve', lambda e: e.memset(ones, 1.0), writes=[B_const])
    P.op('dve', lambda e: e.memset(zero, 0.0), writes=[B_const])
    for h in range(8):
        sc = -8.0 * A_SLOPES[h]
        P.op('dve', (lambda h, sc: lambda e: e.tensor_scalar(out=negsl[:, h, :], in0=cst_s[:, 0:128], scalar1=sc,
                                                            scalar2=None, op0=ALU.mult))(h, sc),
             reads=[B_cst], writes=[B_const])
    P.op('dve', lambda e: e.tensor_copy(out=posq, in_=posi), reads=[B_posi], writes=[B_posq])
    P.op('dve', lambda e: e.tensor_copy(out=poskf, in_=poski), reads=[B_poski], writes=[B_const])
    P.op('dve', lambda e: e.tensor_scalar(out=negposk, in0=poskf, scalar1=-1.0, scalar2=None, op0=ALU.mult),
         reads=[B_const], writes=[B_const])
    P.barrier()

    gmix = AR.view(LOC, [128, D], F32)
    B_gmix = Buf("gmix")
    P.dma('sp', gmix, bass.AP(g_mix.tensor, 0, [[0, 128], [1, D]]), writes=[B_gmix])
    xt = [AR.view(LOC + 4096 + i * 4096, [128, D], F32) for i in range(2)] + [AR.view(LOC + 20480, [128, D], F32)]
    B_xt = [Buf("xt%d" % i) for i in range(3)]
    xn = [AR.view(LOC + 12288 + i * 2048, [128, D], BF16) for i in range(2)] + [AR.view(LOC + 24576, [128, D], BF16)]
    B_xn = [Buf("xn%d" % i) for i in range(3)]
    junk = AR.view(LOC + 16384, [128, D], F32)
    B_junk = Buf("junk")
    psT = [ps[i][:, :].bitcast(BF16).rearrange("p (a b) -> p a b", a=8) for i in range(8)]
    for t in range(32):
        sl = t % 3
        ssb = B_small[sl]
        ss = small[:, sl:sl + 1]
        rs = small[:, 3 + sl:4 + sl]
        rsb = B_small[3 + sl]
        P.dma('sp', xt[sl], x[t * 128:(t + 1) * 128, :], writes=[B_xt[sl]])
        P.op('dve', (lambda sl, ss: lambda e: e.scalar_tensor_tensor(
            out=junk, in0=xt[sl], scalar=1.0, in1=xt[sl], op0=ALU.mult, op1=ALU.mult, accum_out=ss))(sl, ss),
            reads=[B_xt[sl]], writes=[B_junk, ssb])
        P.op('act', (lambda ss, rs: lambda e: e.activation(out=rs, in_=ss, func=AF.Sqrt, bias=EPS, scale=1.0 / D))(ss, rs),
             reads=[ssb], writes=[rsb])
        P.op('dve', (lambda rs: lambda e: e.reciprocal(out=rs, in_=rs))(rs), reads=[rsb], writes=[rsb])
        P.op('dve', (lambda sl, rs: lambda e: e.scalar_tensor_tensor(
            out=xn[sl], in0=xt[sl], scalar=rs, in1=gmix, op0=ALU.mult, op1=ALU.mult))(sl, rs),
            reads=[B_xt[sl], rsb, B_gmix], writes=[B_xn[sl]])
        pb = t % 2
        for c in range(8):
            P.op('pe', (lambda sl, c, pb: lambda e: e.transpose(out=psT[pb][:, c, :], in_=xn[sl][:, c * 128:(c + 1) * 128],
                                                               identity=ident))(sl, c, pb),
                 reads=[B_xn[sl], B_const], writes=[psb[pb]])
        P.op('act', (lambda t, pb: lambda e: e.copy(out=hT[:, :, t * 128:(t + 1) * 128], in_=psT[pb]))(t, pb),
             reads=[psb[pb]], writes=[B_hT[t // 4]])
    P.barrier()

    if stage == 1:
        stg = AR.view(LOC, [128, S], F32)
        B_stg = Buf("stg")
        for c in range(8):
            P.op('dve', (lambda c: lambda e: e.tensor_copy(out=stg, in_=hT[:, c, :]))(c), reads=B_hT, writes=[B_stg])
            P.dma('sp', dbg[:, c, :], stg, reads=[B_stg])
        P.finish()
        P.emit()
        return nc


    wA = [AR.view(LOC + i * 6144, [128, 8, 3, 128], BF16) for i in range(2)]
    B_wA = [[Buf("wA%d_%d" % (i, w)) for w in range(3)] for i in range(2)]
    qkvT = [AR.view(LOC + 12288 + i * 8192, [128, S], BF16) for i in range(3)]
    B_qkv = [Buf("qkvT%d" % i) for i in range(3)]
    Vaug = AR.view(LOC + 36864, [128, 32, 2, 65], BF16)
    B_Vaug = Buf("Vaug")
    dts = [AR.view(LOC + 45312 + i * 512, [128, 256], BF16) for i in range(4)]
    B_dt = [Buf("dt%d" % i) for i in range(4)]
    dabs = [AR.view(5568 + i * 512, [128, 256], BF16) for i in range(4)]
    B_dabs = [Buf("dabs%d" % i) for i in range(4)]
    PTs = [AR.view(LOC + 47360 + i * 512, [128, 256], BF16) for i in range(4)]
    B_PT = [Buf("PT%d" % i) for i in range(4)]
    on = [AR.view(LOC + 49408 + i * 1024, [64, 512], BF16) for i in range(2)]
    B_on = [Buf("on%d" % i) for i in range(2)]
    rec = AR.view(LOC + 51456, [64, 512], F32)
    B_rec = Buf("rec")
    accn = [AR.view(OBT_P2 + i * 16384, [65, S], F32) for i in range(2)]
    B_accn = [Buf("accn%d" % i) for i in range(2)]

    P.op('dve', lambda e: e.memset(Vaug[:, :, :, 64:65], 1.0), writes=[B_Vaug])

    def load_wA(p):
        sl = p % 2
        for w in range(3):
            off = w * 512 + p * 128
            P.dma('pool', wA[sl][:, :, w, :], w_in[:, :, off:off + 128], writes=[B_wA[sl][w]])

    cnt = {'ev': 0, 's': 0, 'dt': 0, 'pt': 0, 'acc': [0, 0]}
    rec2 = AR.view(LOC + 45312, [64, 512], F32)
    recs = [(rec, [B_rec]), (rec2, B_dt)]

    def proj_groups(p):
        sl = p % 2
        out_ = []
        for w in range(3):
            for qb in range(8):
                def grp(w=w, qb=qb, sl=sl):
                    b = qb % 2
                    for c in range(8):
                        P.op('pe', (lambda c: lambda e: e.matmul(
                            out=ps[b][:, :], lhsT=wA[sl][:, c, w, :], rhs=hT[:, c, qb * 512:(qb + 1) * 512],
                            start=(c == 0), stop=(c == 7)))(c),
                            reads=[B_wA[sl][w], B_hT[qb]], writes=[psb[b]])
                    if cnt['ev'] % 2 == 0:
                        P.op('act', lambda e: e.copy(out=qkvT[w][:, qb * 512:(qb + 1) * 512], in_=ps[b][:, :]),
                             reads=[psb[b]], writes=[B_qkv[w]])
                    else:
                        P.op('dve', lambda e: e.tensor_copy(out=qkvT[w][:, qb * 512:(qb + 1) * 512], in_=ps[b][:, :]),
                             reads=[psb[b]], writes=[B_qkv[w]])
                    cnt['ev'] += 1
                out_.append(grp)
        return out_

    def finalize_steps(p):
        f1, f2 = [], []
        for blk in range(8):
            for hh in range(2):
                def s1(blk=blk, hh=hh):
                    cs = slice(blk * 512, (blk + 1) * 512)
                    si = (blk * 2 + hh) % 2
                    sbk = 2 + si
                    rc, brc = recs[si]
                    P.op('pe', lambda e: e.matmul(out=ps[sbk][0:64, :], lhsT=sel[:, :], rhs=accn[hh][:, cs], start=True, stop=True),
                         reads=[B_const, B_accn[hh]], writes=[psb[sbk]])
                    P.op('act', lambda e: e.activation(out=rc[:, :], in_=ps[sbk][0:64, :], func=AF.Ln), reads=[psb[sbk]], writes=brc)
                    P.op('act', lambda e: e.activation(out=rc[:, :], in_=rc[:, :], func=AF.Exp, scale=-1.0), reads=brc, writes=brc)
                    P.op('dve', lambda e: e.tensor_tensor(out=on[hh][:, :], in0=accn[hh][0:64, cs], in1=rc[:, :], op=ALU.mult),
                         reads=[B_accn[hh]] + brc, writes=[B_on[hh]])

                def s2(blk=blk, hh=hh, p=p):
                    cs = slice(blk * 512, (blk + 1) * 512)
                    cb = 4 + blk % 2
                    P.op('pe', lambda e: e.matmul(out=ps[cb][:, :], lhsT=Emat[:, hh, :], rhs=on[hh][:, :], start=(hh == 0), stop=(hh == 1)),
                         reads=[B_const, B_on[hh]], writes=[psb[cb]])
                    if hh == 1:
                        P.op('act', lambda e: e.copy(out=oaT[:, p, cs], in_=ps[cb][:, :]), reads=[psb[cb]], writes=[B_oaT[p][blk]])
                f1.append(s1)
                f2.append(s2)
        return f1, f2

    load_wA(0)
    for dst_, src_, bb_ in ((wg_bf, w_gate, B_wgbf), (wu_bf, w_up, B_wubf)):
        P.dma('pool', dst_.rearrange("p j c f -> (p j) (c f)"), src_.rearrange("p j c f -> (p j) (c f)"), writes=[bb_])
    P.dma('pool', wd_bf.rearrange("p j f -> (p j) f"), w_down.rearrange("p j f -> (p j) f"), writes=[B_wdbf])
    P.dma('pool', wo_bf.rearrange("p j f -> (p j) f"), w_out.rearrange("p j f -> (p j) f"), writes=[B_wobf])
    for bb_ in (B_wgbf, B_wubf, B_wdbf, B_wobf):
        P.pending.pop(id(bb_))
    NPAIR = 4 if stage not in (21, 31) else 1
    for p in range(NPAIR):
        sl = p % 2
        if p + 1 < NPAIR:
            load_wA(p + 1)
        if p == 0:
            for pg in proj_groups(0):
                pg()
        for L in range(3):
            dil = (1, 4, 16)[L]
            Lc = S // dil
            QB = min(512, Lc)
            for j4 in range(8):
                b = 2 + (j4 % 2)
                for jj in range(4):
                    j = j4 * 4 + jj
                    st, step = layout_tokens(L, j)
                    P.op('pe', (lambda b, jj, st, step: lambda e: e.transpose(
                        out=psT[b][:, jj, :], in_=qkvT[2][:, st:st + 127 * step + 1:step], identity=ident))(b, jj, st, step),
                        reads=[B_qkv[2], B_const], writes=[psb[b]])
                P.op('dve', (lambda b, j4: lambda e: e.tensor_copy(
                    out=Vaug[:, j4 * 4:(j4 + 1) * 4, :, 0:64],
                    in_=psT[b][:, 0:4, :].rearrange("p a (h d) -> p a h d", h=2)))(b, j4),
                    reads=[psb[b]], writes=[B_Vaug])
            items = []
            for r in range(dil):
                for Q0 in range(0, Lc, QB):
                    k0s = [k0 for k0 in range(Q0 - 128, Q0 + QB + 1, 128) if 0 <= k0 < Lc]
                    for ki, k0 in enumerate(k0s):
                        items.append((r, Q0, k0, ki == 0, ki == len(k0s) - 1))

            dist_of = {}

            def emit_dist(it, L=L, QB=QB):
                r, Q0, k0, first, last = it
                qlo = max(Q0, k0 - 64)
                qhi = min(Q0 + QB, k0 + 192)
                n = qhi - qlo
                col = L * 32 + (0, r * 8, r * 2)[L] + k0 // 128
                m0 = qlo - (k0 - 64)
                di = cnt['dt'] % 4
                cnt['dt'] += 1
                dist_of[it] = di
                qs = nat_slice(L, r, qlo, qhi)
                if cnt['dt'] % 2 == 0:
                    P.op('act', lambda e: e.activation(out=dabs[di][:, 0:n], in_=posq[:, qs], func=AF.Abs,
                                                       bias=negposk[:, col:col + 1], scale=1.0),
                         reads=[B_posq, B_const], writes=[B_dabs[di]])
                    P.op('dve', lambda e: e.tensor_tensor(out=dts[di][:, 0:n], in0=dabs[di][:, 0:n], in1=maskb[:, m0:m0 + n], op=ALU.max),
                         reads=[B_dabs[di], B_const], writes=[B_dt[di]])
                else:
                    P.op('dve', lambda e: e.tensor_scalar(out=dabs[di][:, 0:n], in0=posq[:, qs], scalar1=negposk[:, col:col + 1],
                                                          scalar2=None, op0=ALU.add),
                         reads=[B_posq, B_const], writes=[B_dabs[di]])
                    P.op('dve', lambda e: e.scalar_tensor_tensor(out=dts[di][:, 0:n], in0=dabs[di][:, 0:n], scalar=-1.0,
                                                                 in1=dabs[di][:, 0:n], op0=ALU.mult, op1=ALU.max),
                         reads=[B_dabs[di]], writes=[B_dt[di]])
                    P.op('dve', lambda e: e.tensor_tensor(out=dts[di][:, 0:n], in0=dts[di][:, 0:n], in1=maskb[:, m0:m0 + n], op=ALU.max),
                         reads=[B_dt[di], B_const], writes=[B_dt[di]])

            def emit_scores(it, L=L, QB=QB, p=p):
                r, Q0, k0, first, last = it
                qlo = max(Q0, k0 - 64)
                qhi = min(Q0 + QB, k0 + 192)
                n = qhi - qlo
                j = (0, r * 8, r * 2)[L] + k0 // 128
                col = L * 32 + j
                m0 = qlo - (k0 - 64)
                di = dist_of[it]
                qs = nat_slice(L, r, qlo, qhi)
                ks = nat_slice(L, r, k0, k0 + 128)
                pis, sbs = [], []
                for hh in range(2):
                    sbs.append(cnt['s'] % 4)
                    cnt['s'] += 1
                    pis.append(cnt['pt'] % 4)
                    cnt['pt'] += 1
                for hh in range(2):
                    sb = sbs[hh]
                    lo, hi = hh * 64, hh * 64 + 64
                    P.op('pe', (lambda sb, lo, hi: lambda e: e.matmul(
                        out=ps[sb][:, 0:n], lhsT=qkvT[1][lo:hi, ks], rhs=qkvT[0][lo:hi, qs], start=True, stop=False))(sb, lo, hi),
                        reads=[B_qkv[0], B_qkv[1]], writes=[psb[sb]])
                for hh in range(2):
                    sb, head = sbs[hh], 2 * p + hh
                    P.op('pe', (lambda sb, head: lambda e: e.matmul(
                        out=ps[sb][:, 0:n], lhsT=negsl[:, head, :], rhs=dts[di][:, 0:n], start=False, stop=True))(sb, head),
                        reads=[B_const, B_dt[di]], writes=[psb[sb]])
                for hh in range(2):
                    sb, pi = sbs[hh], pis[hh]
                    P.op('act', (lambda sb, pi: lambda e: e.activation(
                        out=PTs[pi][:, 0:n], in_=ps[sb][:, 0:n], func=AF.Exp, scale=0.125))(sb, pi),
                        reads=[psb[sb]], writes=[B_PT[pi]])
                return (qlo, qhi, n, j, pis)

            def emit_pv(it, sc, L=L, QB=QB):
                r, Q0, k0, first, last = it
                qlo, qhi, n, j, pis = sc
                if first:
                    accb = []
                    for hh in range(2):
                        ab = 4 + 2 * hh + (cnt['acc'][hh] % 2)
                        cnt['acc'][hh] += 1
                        accb.append(ab)
                        P.op('pe', (lambda ab: lambda e: e.matmul(
                            out=ps[ab][0:65, 0:QB], lhsT=zero[:, 0:65], rhs=zero[:, 0:QB], start=True, stop=False))(ab),
                            reads=[B_const], writes=[psb[ab]])
                    cnt['accb'] = accb
                accb = cnt['accb']
                for hh in range(2):
                    ab = accb[hh]
                    pi = pis[hh]
                    P.op('pe', (lambda ab, hh, pi: lambda e: e.matmul(
                        out=ps[ab][0:65, qlo - Q0:qhi - Q0], lhsT=Vaug[:, j, hh, :], rhs=PTs[pi][:, 0:n],
                        start=False, stop=False, skip_group_check=True))(ab, hh, pi),
                        reads=[B_Vaug, B_PT[pi]], writes=[psb[ab]])
                if last:
                    ns = nat_slice(L, r, Q0, Q0 + QB)
                    for hh in range(2):
                        ab = accb[hh]
                        P.op('pe', (lambda ab: lambda e: e.matmul(
                            out=ps[ab][0:65, 0:QB], lhsT=zero[:, 0:65], rhs=zero[:, 0:QB], start=False, stop=True))(ab),
                            reads=[B_const], writes=[psb[ab]])
                        if L == 0:
                            P.op('dve', (lambda hh, ab: lambda e: e.tensor_copy(
                                out=accn[hh][:, ns], in_=ps[ab][0:65, 0:QB]))(hh, ab),
                                reads=[psb[ab]], writes=[B_accn[hh]])
                        else:
                            P.op('dve', (lambda hh, ab: lambda e: e.tensor_tensor(
                                out=accn[hh][:, ns], in0=accn[hh][:, ns], in1=ps[ab][0:65, 0:QB], op=ALU.add))(hh, ab),
                                reads=[psb[ab], B_accn[hh]], writes=[B_accn[hh]])

            prev = None
            emit_dist(items[0])
            for ii, it in enumerate(items):
                if ii + 1 < len(items):
                    emit_dist(items[ii + 1])
                sc = emit_scores(it)
                if prev is not None:
                    emit_pv(*prev)
                prev = (it, sc)
            emit_pv(*prev)
        f1, f2 = finalize_steps(p)
        pgs = proj_groups(p + 1) if p + 1 < NPAIR else []
        nst = len(f1)
        for i in range(max(len(pgs), nst + 1)):
            if i < len(pgs):
                pgs[i]()
            if i >= 1 and i - 1 < nst:
                f2[i - 1]()
            if i < nst:
                f1[i]()
    P.barrier()

    if stage in (2, 21):
        stg = AR.view(LOC, [128, S], F32)
        B_stg = Buf("stg")
        for c in range(4):
            P.op('dve', (lambda c: lambda e: e.tensor_copy(out=stg, in_=oaT[:, c, :]))(c), reads=B_oaT[c], writes=[B_stg])
            P.dma('sp', dbg[:, c, :], stg, reads=[B_stg])
        P.finish()
        P.emit()
        return nc


    PI = 3.141592653589793
    cosT = AR.view(LOC, [128, S], F32)
    sinT = AR.view(LOC + 16384, [128, S], F32)
    B_cos = [Buf("cos%d" % i) for i in range(8)]
    B_sin = [Buf("sin%d" % i) for i in range(8)]
    WB0 = 188416
    wB = AR.view(WB0, [128, 8, 896], BF16)
    B_wB = Buf("wB")
    sqs = [AR.view(202752 + i * 1024, [128, 512], BF16) for i in range(2)]
    B_sq = [Buf("sq%d" % i) for i in range(2)]
    rstd = AR.view(204800, [128, 512], F32)
    B_rstd = Buf("rstd")
    TB_ = AR.view(206848, [128, 512], F32)
    TA_ = AR.view(208896, [128, 512], F32)
    TI_ = AR.view(81920, [128, 512], I32)
    B_TA, B_TB, B_TI = Buf("TA"), Buf("TB"), Buf("TI")
    invf = gc[:, 13:14]
    sgn = gc[:, 14:15]
    P.dma('pool', wB, w_in[:, :, 1536:2432], writes=[B_wB])
    for blk in range(8):
        cs = slice(blk * 512, (blk + 1) * 512)
        P.op('dve', (lambda cs: lambda e: e.tensor_scalar(out=TA_, in0=posq[:, cs], scalar1=invf, scalar2=None, op0=ALU.mult))(cs),
             reads=[B_posq, B_gc], writes=[B_TA])
        P.op('dve', lambda e: e.tensor_scalar(out=TI_, in0=TA_, scalar1=1.0 / (2 * PI), scalar2=None, op0=ALU.mult),
             reads=[B_TA], writes=[B_TI])
        P.op('dve', lambda e: e.tensor_copy(out=TB_, in_=TI_), reads=[B_TI], writes=[B_TB])
        P.op('dve', lambda e: e.scalar_tensor_tensor(out=TA_, in0=TB_, scalar=-2 * PI, in1=TA_, op0=ALU.mult, op1=ALU.add),
             reads=[B_TB, B_TA], writes=[B_TA])
        P.op('dve', lambda e: e.tensor_scalar(out=TB_, in0=TA_, scalar1=PI, scalar2=-2 * PI, op0=ALU.is_gt, op1=ALU.mult),
             reads=[B_TA], writes=[B_TB])
        P.op('dve', lambda e: e.tensor_tensor(out=TA_, in0=TA_, in1=TB_, op=ALU.add), reads=[B_TA, B_TB], writes=[B_TA])
        P.op('dve', lambda e: e.tensor_scalar(out=TB_, in0=TA_, scalar1=-PI, scalar2=2 * PI, op0=ALU.is_lt, op1=ALU.mult),
             reads=[B_TA], writes=[B_TB])
        P.op('dve', lambda e: e.tensor_tensor(out=TA_, in0=TA_, in1=TB_, op=ALU.add), reads=[B_TA, B_TB], writes=[B_TA])
        P.op('dve', lambda e: e.tensor_scalar(out=TA_, in0=TA_, scalar1=3.14159, scalar2=-3.14159, op0=ALU.min, op1=ALU.max),
             reads=[B_TA], writes=[B_TA])
        P.op('act', (lambda cs: lambda e: e.activation(out=sinT[:, cs], in_=TA_, func=AF.Sin, scale=sgn))(cs),
             reads=[B_TA, B_gc], writes=[B_sin[blk]])
        P.op('act', lambda e: e.activation(out=TB_, in_=TA_, func=AF.Abs), reads=[B_TA], writes=[B_TB])
        P.op('act', (lambda cs: lambda e: e.activation(out=cosT[:, cs], in_=TB_, func=AF.Sin, scale=-1.0, bias=PI / 2))(cs),
             reads=[B_TB], writes=[B_cos[blk]])

    ckvnT = AR.view(8192, [128, 2, S], BF16)
    cqnT = AR.view(57344, [128, 3, S], BF16)
    kpeT = AR.view(81920, [128, S], BF16)
    B_cqn = [Buf("cqn%d" % i) for i in range(8)]
    B_ckvn = [Buf("ckvn%d" % i) for i in range(8)]
    B_kpe = [Buf("kpe%d" % i) for i in range(8)]

    def rope(qb, psa, psb_a, psc, psb_c, dst, B_dst):
        cs = slice(qb * 512, (qb + 1) * 512)
        P.op('dve', (lambda cs: lambda e: e.tensor_tensor(out=TA_, in0=psa, in1=cosT[:, cs], op=ALU.mult))(cs),
             reads=[psb_a, B_cos[qb]], writes=[B_TA])
        P.op('dve', (lambda cs: lambda e: e.tensor_tensor(out=TB_, in0=psc, in1=sinT[:, cs], op=ALU.mult))(cs),
             reads=[psb_c, B_sin[qb]], writes=[B_TB])
        P.op('dve', (lambda cs: lambda e: e.tensor_tensor(out=dst[:, cs], in0=TA_, in1=TB_, op=ALU.add))(cs),
             reads=[B_TA, B_TB], writes=[B_dst] + ([B_TI] if dst is kpeT else []))

    rr = {'b': 0, 'sq': 0, 'ev': 0}
    RAWB = (0, 1, 6, 7)

    def evac(dst_ap, src_ap, reads, writes):
        if rr['ev'] % 2 == 0:
            P.op('act', lambda e: e.copy(out=dst_ap, in_=src_ap), reads=reads, writes=writes)
        else:
            P.op('dve', lambda e: e.tensor_copy(out=dst_ap, in_=src_ap), reads=reads, writes=writes)
        rr['ev'] += 1

    deferred = []

    def flush():
        for f_ in deferred:
            f_()
        del deferred[:]

    for qb in range(8):
        cs = slice(qb * 512, (qb + 1) * 512)
        for (bank, col0) in ((4, 640), (5, 768)):
            for c in range(8):
                P.op('pe', (lambda bank, c, col0, cs: lambda e: e.matmul(
                    out=ps[bank][:, :], lhsT=wB[:, c, col0:col0 + 128], rhs=hT[:, c, cs],
                    start=(c == 0), stop=(c == 7)))(bank, c, col0, cs),
                    reads=[B_wB, B_hT[qb]], writes=[psb[bank]])
        flush()
        rope(qb, ps[4][:, :], psb[4], ps[5][:, :], psb[5], kpeT, B_kpe[qb])
        for (dst, Bd, nch, col0, gcol0, sbank) in ((cqnT, B_cqn, 3, 0, 0, 2), (ckvnT, B_ckvn, 2, 384, 3, 3)):
            for i in range(nch):
                b = RAWB[rr['b'] % 4]
                rr['b'] += 1
                for c in range(8):
                    P.op('pe', (lambda b, c, i, col0, cs: lambda e: e.matmul(
                        out=ps[b][:, :], lhsT=wB[:, c, col0 + i * 128:col0 + (i + 1) * 128], rhs=hT[:, c, cs],
                        start=(c == 0), stop=(c == 7)))(b, c, i, col0, cs),
                        reads=[B_wB, B_hT[qb]], writes=[psb[b]])
                flush()
                P.op('act', (lambda dst, i, cs, b: lambda e: e.copy(out=dst[:, i, cs], in_=ps[b][:, :]))(dst, i, cs, b),
                     reads=[psb[b]], writes=[Bd[qb]] + ([B_posq] if dst is ckvnT else []))
                si = rr['sq'] % 2
                rr['sq'] += 1
                P.op('act', (lambda si, b: lambda e: e.activation(out=sqs[si], in_=ps[b][:, :], func=AF.Square))(si, b),
                     reads=[psb[b]], writes=[B_sq[si]])

                def ones_mm(sbank=sbank, si=si, i=i, nch=nch):
                    P.op('pe', lambda e: e.matmul(out=ps[sbank][:, :], lhsT=ones, rhs=sqs[si], start=(i == 0), stop=(i == nch - 1)),
                         reads=[B_const, B_sq[si]], writes=[psb[sbank]])
                deferred.append(ones_mm)

            def norm_tail(dst=dst, Bd=Bd, nch=nch, gcol0=gcol0, sbank=sbank, qb=qb, cs=cs):
                nf = nch * 128
                P.op('act', lambda e: e.activation(out=rstd, in_=ps[sbank][:, :], func=AF.Ln, bias=EPS, scale=1.0 / nf),
                     reads=[psb[sbank]], writes=[B_rstd])
                P.op('act', lambda e: e.activation(out=rstd, in_=rstd, func=AF.Exp, scale=-0.5), reads=[B_rstd], writes=[B_rstd])
                for i in range(nch):
                    P.op('dve', (lambda i: lambda e: e.scalar_tensor_tensor(
                        out=dst[:, i, cs], in0=dst[:, i, cs], scalar=gc[:, gcol0 + i:gcol0 + i + 1], in1=rstd,
                        op0=ALU.mult, op1=ALU.mult))(i),
                        reads=[Bd[qb], B_gc, B_rstd], writes=[Bd[qb]])
            deferred.append(norm_tail)
    flush()
    P.barrier()

    OBT = 90112
    obT = AR.view(OBT, [128, 4, S], BF16)
    B_obT = [[Buf("obT%d_%d" % (c, q)) for q in range(8)] for c in range(4)]
    qnT = AR.view(122880, [128, S], BF16)
    qpeT = AR.view(131072, [128, S], BF16)
    knT = AR.view(139264, [128, S], BF16)
    vtok = AR.view(147456, [128, 32, 128], BF16)
    B_qn = [Buf("qn%d" % i) for i in range(8)]
    B_qpe = [Buf("qpe%d" % i) for i in range(8)]
    B_kn = [Buf("kn%d" % i) for i in range(8)]
    B_vt = [Buf("vt%d" % i) for i in range(8)]
    wq = [AR.view(WB0 + i * 3328, [128, 3, 384], BF16) for i in range(2)]
    wkv = [AR.view(WB0 + i * 3328 + 2304, [128, 2, 256], BF16) for i in range(2)]
    B_wq = [Buf("wq%d" % i) for i in range(2)]
    B_wkv = [Buf("wkv%d" % i) for i in range(2)]
    PT5 = [AR.view(WB0 + 6656 + i * 1024, [128, 512], BF16) for i in range(4)]
    B_PT5 = [Buf("PT5_%d" % i) for i in range(4)]
    rec5 = AR.view(WB0 + 10752, [128, 512], F32)
    B_rec5 = Buf("rec5")
    accD = [AR.view(WB0 + 12800 + i * 2048, [128, 512], F32) for i in range(2)]
    B_accD = [Buf("accD%d" % i) for i in range(2)]
    ones32 = AR.view(WB0 + 16896, [128, 128], F32)
    P.op('dve', lambda e: e.memset(ones32, 1.0), writes=[B_const])

    def load_wh(h):
        sl = h % 2
        P.dma('pool', wq[sl], w_qup[:, :, h * 384:(h + 1) * 384], writes=[B_wq[sl]])
        P.dma('pool', wkv[sl], w_kvup[:, :, h * 256:(h + 1) * 256], writes=[B_wkv[sl]])

    NH = 4 if stage != 31 else 1
    load_wh(0)
    SC_B = 192.0 ** -0.5
    cnt5 = {'s': 0, 'pt': 0}
    for h in range(NH):
        sl = h % 2
        if h + 1 < NH:
            load_wh(h + 1)
        for qb in range(8):
            cs = slice(qb * 512, (qb + 1) * 512)
            b = rr['b'] % 4
            rr['b'] += 1
            for c in range(3):
                P.op('pe', (lambda b, c, cs, sl: lambda e: e.matmul(out=ps[b][:, :], lhsT=wq[sl][:, c, 0:128], rhs=cqnT[:, c, cs],
                                                                   start=(c == 0), stop=(c == 2)))(b, c, cs, sl),
                     reads=[B_wq[sl], B_cqn[qb]], writes=[psb[b]])
            evac(qnT[:, cs], ps[b][:, :], [psb[b]], [B_qn[qb]])
            rb = 4 + 2 * (qb % 2)
            for (bank, col0) in ((rb, 128), (rb + 1, 256)):
                for c in range(3):
                    P.op('pe', (lambda bank, c, col0, cs, sl: lambda e: e.matmul(
                        out=ps[bank][:, :], lhsT=wq[sl][:, c, col0:col0 + 128], rhs=cqnT[:, c, cs],
                        start=(c == 0), stop=(c == 2)))(bank, c, col0, cs, sl),
                        reads=[B_wq[sl], B_cqn[qb]], writes=[psb[bank]])
            rope(qb, ps[rb][:, :], psb[rb], ps[rb + 1][:, :], psb[rb + 1], qpeT, B_qpe[qb])
            b = rr['b'] % 4
            rr['b'] += 1
            for c in range(2):
                P.op('pe', (lambda b, c, cs, sl: lambda e: e.matmul(out=ps[b][:, :], lhsT=wkv[sl][:, c, 0:128], rhs=ckvnT[:, c, cs],
                                                                   start=(c == 0), stop=(c == 1)))(b, c, cs, sl),
                     reads=[B_wkv[sl], B_ckvn[qb]], writes=[psb[b]])
            evac(knT[:, cs], ps[b][:, :], [psb[b]], [B_kn[qb]])
            b = rr['b'] % 4
            rr['b'] += 1
            for tt in range(4):
                ts_ = slice(qb * 512 + tt * 128, qb * 512 + (tt + 1) * 128)
                for c in range(2):
                    P.op('pe', (lambda b, c, tt, ts_, sl: lambda e: e.matmul(
                        out=ps[b][:, tt * 128:(tt + 1) * 128], lhsT=ckvnT[:, c, ts_], rhs=wkv[sl][:, c, 128:256],
                        start=(c == 0), stop=(c == 1)))(b, c, tt, ts_, sl),
                        reads=[B_wkv[sl], B_ckvn[qb]], writes=[psb[b]])
            evac(vtok[:, qb * 4:(qb + 1) * 4, :], ps[b][:, :].rearrange("p (a b) -> p a b", a=4), [psb[b]], [B_vt[qb]])
        units = [(qb, kp) for qb in range(8) for kp in range(16)]

        def emit_S(u, h=h):
            qb, kp = u
            cs = slice(qb * 512, (qb + 1) * 512)
            banks = (0, 1) if cnt5['s'] % 2 == 0 else (2, 3)
            cnt5['s'] += 1
            pis = (cnt5['pt'] % 4, (cnt5['pt'] + 1) % 4)
            cnt5['pt'] += 2
            for t in range(2):
                kt = kp * 2 + t
                ks = slice(kt * 128, (kt + 1) * 128)
                P.op('pe', (lambda t, ks: lambda e: e.matmul(out=ps[banks[t]][:, :], lhsT=knT[:, ks], rhs=qnT[:, cs],
                                                            start=True, stop=False))(t, ks),
                     reads=[B_kn[kt // 4], B_qn[qb]], writes=[psb[banks[t]]])
            for t in range(2):
                kt = kp * 2 + t
                ks = slice(kt * 128, (kt + 1) * 128)
                lo = t * 64
                P.op('pe', (lambda t, ks, lo: lambda e: e.matmul(out=ps[banks[t]][:, :], lhsT=kpeT[lo:lo + 64, ks], rhs=qpeT[lo:lo + 64, cs],
                                                                start=False, stop=True))(t, ks, lo),
                     reads=[B_kpe[kt // 4], B_qpe[qb]], writes=[psb[banks[t]]])
            for t in range(2):
                P.op('act', (lambda t: lambda e: e.activation(out=PT5[pis[t]], in_=ps[banks[t]][:, :], func=AF.Exp, scale=SC_B))(t),
                     reads=[psb[banks[t]]], writes=[B_PT5[pis[t]]])
            return pis

        def emit_PV(u, pis, h=h):
            qb, kp = u
            cs = slice(qb * 512, (qb + 1) * 512)
            nb, db = 4 + qb % 2, 6 + qb % 2
            for t in range(2):
                kt = kp * 2 + t
                pi = pis[t]
                P.op('pe', (lambda kt, pi: lambda e: e.matmul(out=ps[nb][:, :], lhsT=vtok[:, kt, :], rhs=PT5[pi],
                                                             start=(kt == 0), stop=(kt == 31)))(kt, pi),
                     reads=[B_vt[kt // 4], B_PT5[pi]], writes=[psb[nb]])
                ad, bad = accD[qb % 2], B_accD[qb % 2]
                if t == 0:
                    P.op('pe', (lambda kt, pi: lambda e: e.matmul(out=ps[db][:, :], lhsT=ones, rhs=PT5[pi],
                                                                 start=(kt == 0), stop=False))(kt, pi),
                         reads=[B_const, B_PT5[pi]], writes=[psb[db]])
                elif kt == 1:
                    P.op('dve', (lambda pi: lambda e: e.tensor_copy(out=ad, in_=PT5[pi]))(pi), reads=[B_PT5[pi]], writes=[bad])
                else:
                    P.op('dve', (lambda pi: lambda e: e.tensor_tensor(out=ad, in0=ad, in1=PT5[pi], op=ALU.add))(pi),
                         reads=[B_PT5[pi], bad], writes=[bad])
            if kp == 15:
                ad, bad = accD[qb % 2], B_accD[qb % 2]
                P.op('pe', lambda e: e.matmul(out=ps[db][:, :], lhsT=ones32, rhs=ad, start=False, stop=True),
                     reads=[B_const, bad], writes=[psb[db]])
                P.op('act', lambda e: e.activation(out=rec5, in_=ps[db][:, :], func=AF.Ln), reads=[psb[db]], writes=[B_rec5])
                P.op('act', lambda e: e.activation(out=rec5, in_=rec5, func=AF.Exp, scale=-1.0), reads=[B_rec5], writes=[B_rec5])
                P.op('dve', lambda e: e.tensor_tensor(out=obT[:, h, cs], in0=ps[nb][:, :], in1=rec5, op=ALU.mult),
                     reads=[psb[nb], B_rec5], writes=[B_obT[h][qb]])

        prev = None
        for u in units:
            pis = emit_S(u)
            if prev is not None:
                emit_PV(*prev)
            prev = (u, pis)
        emit_PV(*prev)
    P.barrier()

    if stage in (3, 31):
        stg = AR.view(LOC, [128, S], F32)
        B_stg = Buf("stg")
        for c in range(4):
            P.op('dve', (lambda c: lambda e: e.tensor_copy(out=stg, in_=obT[:, c, :]))(c), reads=B_obT[c], writes=[B_stg])
            P.dma('sp', dbg[:, c, :], stg, reads=[B_stg])
        P.finish()
        P.emit()
        return nc


    def mixT(c):
        return (oaT[:, c, :], B_oaT[c]) if c < 4 else (obT[:, c - 4, :], B_obT[c - 4])

    x1 = AR.view(8192, [128, 4, D], F32)
    B_x1 = [Buf("x1_%d" % i) for i in range(4)]
    wmq = AR.view(57344, [128, 8, 512], BF16)
    wmo = AR.view(65536, [128, 4, D], BF16)
    WDs = [AR.view(73728, [128, 8, D], BF16), AR.view(194560, [128, 8, D], BF16)]
    B_WDs = [Buf("WD0"), Buf("WD1")]
    B_wmq, B_wmo = Buf("wmq"), Buf("wmo")
    T0 = 122880
    gmq = AR.view(T0, [128, D], F32)
    gff = AR.view(T0 + 4096, [128, D], F32)
    gfi = AR.view(T0 + 8192, [128, D], F32)
    B_g3 = Buf("g3")
    mkkT = AR.view(135168, [128, 4, 256], BF16)
    mvv = AR.view(137216, [128, 2, 512], BF16)
    B_mkk, B_mvv = Buf("mkk"), Buf("mvv")
    xl = [AR.view(139264 + i * 4096, [128, D], F32) for i in range(2)]
    B_xl = [Buf("xl%d" % i) for i in range(2)]
    xn2 = [AR.view(147456 + i * 2048, [128, D], BF16) for i in range(2)]
    B_xn2 = [Buf("xn2_%d" % i) for i in range(2)]
    hg = AR.view(151552, [128, 8, 512], BF16)
    B_hg = [Buf("hg%d" % i) for i in range(4)]
    mqT = AR.view(159744, [128, 4, 512], BF16)
    moT = mqT
    B_mq = [Buf("mq%d" % i) for i in range(4)]
    B_mo = B_mq
    PT6 = [AR.view(167936 + i * 1024, [128, 512], BF16) for i in range(2)]
    B_PT6 = [Buf("PT6_%d" % i) for i in range(2)]
    rec6 = AR.view(169984, [128, 512], F32)
    B_rec6 = Buf("rec6")
    actT = AR.view(172032, [128, 8, 512], BF16)
    B_act = [Buf("act%d" % i) for i in range(8)]
    gtmp = [AR.view(180224 + i * 2048, [128, 512], F32) for i in range(2)]
    B_gtmp = [Buf("gtmp%d" % i) for i in range(2)]
    xn2 = xn2 + [AR.view(180224 + i * 2048, [128, D], BF16) for i in range(2)]
    B_xn2 = B_xn2 + B_gtmp
    WGO = [184320, 188416, 163840]
    wg = [AR.view(WGO[i], [128, 8, 128], BF16) for i in range(3)]
    wu = [AR.view(WGO[i] + 2048, [128, 8, 128], BF16) for i in range(3)]
    B_wg = [Buf("wg%d" % i) for i in range(3)]
    B_wu = [Buf("wu%d" % i) for i in range(3)]
    junk2 = AR.view(192512, [128, D], BF16)
    B_junk2 = Buf("junk2")
    tl = {'x': 0, 'x4': 0, 'st': 0, 'b': 0, 'pT': 0, 's': 0, 'pt': 0, 'g': 0, 'o': 0, 'w': 0, 'wd': 0}

    def rms_stats4(srcs):
        n = len(srcs)
        k = tl['st'] % 6
        tl['st'] += 1
        blk = small[:, 8 + 4 * k:8 + 4 * k + n]
        bb = B_small[8 + k]
        for i, (ap_, bufs_) in enumerate(srcs):
            col = small[:, 8 + 4 * k + i:8 + 4 * k + i + 1]
            P.op('dve', (lambda ap_, col: lambda e: e.scalar_tensor_tensor(out=junk2, in0=ap_, scalar=1.0, in1=ap_, op0=ALU.mult,
                                                                          op1=ALU.mult, accum_out=col))(ap_, col),
                 reads=bufs_, writes=[B_junk2, bb])
        P.op('act', lambda e: e.activation(out=blk, in_=blk, func=AF.Sqrt, bias=EPS, scale=1.0 / D), reads=[bb], writes=[bb])
        P.op('dve', lambda e: e.reciprocal(out=blk, in_=blk), reads=[bb], writes=[bb])
        return [small[:, 8 + 4 * k + i:8 + 4 * k + i + 1] for i in range(n)], bb

    def norm_T4(srcs, gtile, dstT, dst_bufs):
        cols, bb = rms_stats4(srcs)
        for i, (ap_, bufs_) in enumerate(srcs):
            k = tl['x'] % 2
            tl['x'] += 1
            P.op('dve', (lambda ap_, i, k: lambda e: e.scalar_tensor_tensor(out=xn2[k], in0=ap_, scalar=cols[i], in1=gtile,
                                                                           op0=ALU.mult, op1=ALU.mult))(ap_, i, k),
                 reads=list(bufs_) + [bb, B_g3], writes=[B_xn2[k]])
            pb = 4 + tl['pT'] % 2
            tl['pT'] += 1
            for c in range(8):
                P.op('pe', (lambda c, k, pb: lambda e: e.transpose(out=psT[pb][:, c, :], in_=xn2[k][:, c * 128:(c + 1) * 128],
                                                                  identity=ident))(c, k, pb),
                     reads=[B_xn2[k], B_const], writes=[psb[pb]])
            P.op('act', (lambda i, pb: lambda e: e.copy(out=dstT[:, :, i * 128:(i + 1) * 128], in_=psT[pb]))(i, pb),
                 reads=[psb[pb]], writes=[dst_bufs[i]])

    def act_stats(ap_, bufs_):
        k = tl['st'] % 6
        tl['st'] += 1
        col = small[:, 8 + 4 * k:8 + 4 * k + 1]
        bb = B_small[8 + k]
        P.op('act', lambda e: e.activation(out=junk2, in_=ap_, func=AF.Square, accum_out=col), reads=bufs_, writes=[B_junk2, bb])
        P.op('act', lambda e: e.activation(out=col, in_=col, func=AF.Sqrt, bias=EPS, scale=1.0 / D), reads=[bb], writes=[bb])
        P.op('dve', lambda e: e.reciprocal(out=col, in_=col), reads=[bb], writes=[bb])
        return col, bb

    def norm_pre(ap_, bufs_, gtile):
        col, bb = act_stats(ap_, bufs_)
        k = tl['x4'] % 4
        tl['x4'] += 1
        P.op('dve', lambda e: e.scalar_tensor_tensor(out=xn2[k], in0=ap_, scalar=col, in1=gtile, op0=ALU.mult, op1=ALU.mult),
             reads=list(bufs_) + [bb, B_g3], writes=[B_xn2[k]])
        return k

    def norm_post(k, dstT, i, dst_buf, g0):
        pb = 4 + tl['pT'] % 4
        tl['pT'] += 1
        for c in range(8):
            P.op('pe', (lambda c: lambda e: e.transpose(out=psT[pb][:, c, :], in_=xn2[k][:, c * 128:(c + 1) * 128], identity=ident))(c),
                 reads=[B_xn2[k], B_const], writes=[psb[pb]])
        P.op('act', lambda e: e.copy(out=dstT[:, :, i * 128:(i + 1) * 128], in_=psT[pb]), reads=[psb[pb]], writes=[dst_buf])

    def norm_T1(ap_, bufs_, gtile, dstT, i, dst_buf):
        cols, bb = rms_stats4([(ap_, bufs_)])
        k = tl['x'] % 2
        tl['x'] += 1
        P.op('dve', lambda e: e.scalar_tensor_tensor(out=xn2[k], in0=ap_, scalar=cols[0], in1=gtile, op0=ALU.mult, op1=ALU.mult),
             reads=list(bufs_) + [bb, B_g3], writes=[B_xn2[k]])
        pb = 4 + tl['pT'] % 2
        tl['pT'] += 1
        for c in range(8):
            P.op('pe', (lambda c: lambda e: e.transpose(out=psT[pb][:, c, :], in_=xn2[k][:, c * 128:(c + 1) * 128], identity=ident))(c),
                 reads=[B_xn2[k], B_const], writes=[psb[pb]])
        P.op('act', lambda e: e.copy(out=dstT[:, :, i * 128:(i + 1) * 128], in_=psT[pb]), reads=[psb[pb]], writes=[dst_buf])

    def nextb():
        b = tl['b'] % 4
        tl['b'] += 1
        return b

    gkvb = AR.view(172032, [128, D], F32)
    wmkv = WDs[0]
    mkT = AR.view(151552, [128, 8, 256], BF16)
    B_mkT = [Buf("mkT%d" % i) for i in range(2)]
    P.dma('sp', gkvb, bass.AP(g_memkv.tensor, 0, [[0, 128], [1, D]]), writes=[B_g3])
    P.dma('pool', wmkv, w_mkv, writes=[B_WDs[0]], sembuf=Buf("wmkv_sem"))
    for mt in range(2):
        P.dma('sp', xl[mt], mem[mt * 128:(mt + 1) * 128, :], writes=[B_xl[mt]])
    for gt, src in ((gmq, g_memq), (gff, g_ffn), (gfi, g_fin)):
        P.dma('sp', gt, bass.AP(src.tensor, 0, [[0, 128], [1, D]]), writes=[B_g3])
    P.dma('pool', wmq, w_mq, writes=[B_wmq])
    P.dma('pool', wmo, w_mo, writes=[B_wmo])
    p4def = []
    p4i = 0
    for (T_, Bt, gcol0) in ((oaT, B_oaT, 5), (obT, B_obT, 9)):
        for qb in range(8):
            cs = slice(qb * 512, (qb + 1) * 512)
            sbank = 2 + p4i % 2
            p4i += 1
            for c in range(4):
                si = rr['sq'] % 2
                rr['sq'] += 1
                P.op('act', (lambda T_, c, cs, si: lambda e: e.activation(out=sqs[si], in_=T_[:, c, cs], func=AF.Square))(T_, c, cs, si),
                     reads=[Bt[c][qb]], writes=[B_sq[si]])
                P.op('pe', (lambda si, c, sbank: lambda e: e.matmul(out=ps[sbank][:, :], lhsT=ones, rhs=sqs[si], start=(c == 0), stop=(c == 3)))(si, c, sbank),
                     reads=[B_const, B_sq[si]], writes=[psb[sbank]])
            for f_ in p4def:
                f_()
            del p4def[:]

            def tail4(T_=T_, Bt=Bt, gcol0=gcol0, qb=qb, cs=cs, sbank=sbank):
                P.op('act', lambda e: e.activation(out=rstd, in_=ps[sbank][:, :], func=AF.Ln, bias=EPS, scale=1.0 / 512),
                     reads=[psb[sbank]], writes=[B_rstd])
                P.op('act', lambda e: e.activation(out=rstd, in_=rstd, func=AF.Exp, scale=-0.5), reads=[B_rstd], writes=[B_rstd])
                for c in range(4):
                    P.op('dve', (lambda c: lambda e: e.scalar_tensor_tensor(
                        out=T_[:, c, cs], in0=T_[:, c, cs], scalar=gc[:, gcol0 + c:gcol0 + c + 1], in1=rstd,
                        op0=ALU.mult, op1=ALU.mult))(c),
                        reads=[Bt[c][qb], B_gc, B_rstd], writes=[Bt[c][qb]])
            p4def.append(tail4)
    for f_ in p4def:
        f_()
    norm_T4([(xl[mt], [B_xl[mt]]) for mt in range(2)], gkvb, mkT, B_mkT)
    for h in range(4):
        b = nextb()
        for c in range(8):
            P.op('pe', (lambda b, c, h: lambda e: e.matmul(out=ps[b][:, 0:256], lhsT=wmkv[:, c, h * 128:(h + 1) * 128], rhs=mkT[:, c, :],
                                                          start=(c == 0), stop=(c == 7)))(b, c, h),
                 reads=[B_WDs[0]] + B_mkT, writes=[psb[b]])
        P.op('dve', (lambda b, h: lambda e: e.tensor_copy(out=mkkT[:, h, :], in_=ps[b][:, 0:256]))(b, h), reads=[psb[b]], writes=[B_mkk])
    for mt in range(2):
        b = nextb()
        for c in range(8):
            P.op('pe', (lambda b, c, mt: lambda e: e.matmul(out=ps[b][:, :], lhsT=mkT[:, c, mt * 128:(mt + 1) * 128], rhs=wmkv[:, c, 512:1024],
                                                           start=(c == 0), stop=(c == 7)))(b, c, mt),
                 reads=[B_WDs[0]] + B_mkT, writes=[psb[b]])
        P.op('dve', (lambda b, mt: lambda e: e.tensor_copy(out=mvv[:, mt, :], in_=ps[b][:, :]))(b, mt), reads=[psb[b]], writes=[B_mvv])
    P.barrier()

    SC_M = 128.0 ** -0.5
    NG = 8 if stage != 61 else 1
    THIRDS = ((0, 8), (8, 8), (16, 6))

    def next_wd():
        k = tl['wd'] % 2
        tl['wd'] += 1
        return k

    pre = {}

    def prefetch_group(g):
        wk = next_wd()
        P.dma('sp', WDs[wk], wo_bf, reads=[B_wobf], writes=[B_WDs[wk]])
        xs = []
        for tt in range(2):
            k = tl['o'] % 2
            tl['o'] += 1
            P.dma('sp', xl[k], x[(g * 4 + tt) * 128:(g * 4 + tt + 1) * 128, :], writes=[B_xl[k]])
            xs.append(k)
        pre[g] = (wk, xs)

    prefetch_group(0)
    for g in range(NG):
        wk, xs = pre[g]
        WDv, B_WD = WDs[wk], B_WDs[wk]
        nslots = []
        for tt in range(4):
            T = g * 4 + tt
            if tt < 2:
                k = xs[tt]
            else:
                k = tl['o'] % 2
                tl['o'] += 1
                P.dma('sp', xl[k], x[T * 128:(T + 1) * 128, :], writes=[B_xl[k]])
            for half in range(2):
                hc = slice(half * 512, (half + 1) * 512)
                b = nextb()
                for c in range(8):
                    mt_, mb_ = mixT(c)
                    P.op('pe', (lambda b, c, mt_, T, hc, WDv: lambda e: e.matmul(out=ps[b][:, :], lhsT=mt_[:, T * 128:(T + 1) * 128],
                                                                                rhs=WDv[:, c, hc], start=(c == 0), stop=(c == 7)))(b, c, mt_, T, hc, WDv),
                         reads=[mb_[g], B_WD], writes=[psb[b]])
                P.op('dve', (lambda b, tt, hc, k: lambda e: e.tensor_tensor(out=x1[:, tt, hc], in0=ps[b][:, :], in1=xl[k][:, hc], op=ALU.add))(b, tt, hc, k),
                     reads=[psb[b], B_xl[k]], writes=[B_x1[tt]])
            nslots.append(norm_pre(x1[:, tt, :], [B_x1[tt]], gmq))
        for tt in range(4):
            norm_post(nslots[tt], hg, tt, B_hg[tt], 0)
        for h in range(4):
            b = nextb()
            for c in range(8):
                P.op('pe', (lambda b, c, h: lambda e: e.matmul(out=ps[b][:, :], lhsT=wmq[:, c, h * 128:(h + 1) * 128], rhs=hg[:, c, :],
                                                              start=(c == 0), stop=(c == 7)))(b, c, h),
                     reads=[B_wmq] + B_hg, writes=[psb[b]])
            P.op('act', (lambda b, h: lambda e: e.copy(out=mqT[:, h, :], in_=ps[b][:, :]))(b, h), reads=[psb[b]], writes=[B_mq[h]])
        units6 = [(h, mc) for h in range(4) for mc in range(2)]

        def emit_S6(u):
            h, mc = u
            sb = nextb()
            pi = tl['pt'] % 2
            tl['pt'] += 1
            P.op('pe', lambda e: e.matmul(out=ps[sb][:, :], lhsT=mkkT[:, h, mc * 128:(mc + 1) * 128], rhs=mqT[:, h, :],
                                          start=True, stop=True),
                 reads=[B_mkk, B_mq[h]], writes=[psb[sb]])
            P.op('act', lambda e: e.activation(out=PT6[pi], in_=ps[sb][:, :], func=AF.Exp, scale=SC_M),
                 reads=[psb[sb]], writes=[B_PT6[pi]])
            return pi

        def emit_PV6(u, pi):
            h, mc = u
            nb, db = (6, 7) if h % 2 == 0 else (4, 5)
            P.op('pe', lambda e: e.matmul(out=ps[nb][:, :], lhsT=mvv[:, mc, h * 128:(h + 1) * 128], rhs=PT6[pi],
                                          start=(mc == 0), stop=(mc == 1)),
                 reads=[B_mvv, B_PT6[pi]], writes=[psb[nb]])
            P.op('pe', lambda e: e.matmul(out=ps[db][:, :], lhsT=ones, rhs=PT6[pi], start=(mc == 0), stop=(mc == 1)),
                 reads=[B_const, B_PT6[pi]], writes=[psb[db]])
            if mc == 1:
                P.op('act', lambda e: e.activation(out=rec6, in_=ps[db][:, :], func=AF.Ln), reads=[psb[db]], writes=[B_rec6])
                P.op('act', lambda e: e.activation(out=rec6, in_=rec6, func=AF.Exp, scale=-1.0), reads=[B_rec6], writes=[B_rec6])
                P.op('dve', lambda e: e.tensor_tensor(out=moT[:, h, :], in0=ps[nb][:, :], in1=rec6, op=ALU.mult),
                     reads=[psb[nb], B_rec6], writes=[B_mo[h]])

        prev6 = None
        for u in units6:
            pi = emit_S6(u)
            if prev6 is not None:
                emit_PV6(*prev6)
            prev6 = (u, pi)
        emit_PV6(*prev6)
        nslots = []
        for tt in range(4):
            for half in range(2):
                hc = slice(half * 512, (half + 1) * 512)
                b = nextb()
                for h in range(4):
                    P.op('pe', (lambda b, h, tt, hc: lambda e: e.matmul(out=ps[b][:, :], lhsT=moT[:, h, tt * 128:(tt + 1) * 128], rhs=wmo[:, h, hc],
                                                                       start=(h == 0), stop=(h == 3)))(b, h, tt, hc),
                         reads=[B_mo[h], B_wmo], writes=[psb[b]])
                P.op('dve', (lambda b, tt, hc: lambda e: e.tensor_tensor(out=x1[:, tt, hc], in0=ps[b][:, :], in1=x1[:, tt, hc], op=ALU.add))(b, tt, hc),
                     reads=[psb[b], B_x1[tt]], writes=[B_x1[tt]])
            nslots.append(norm_pre(x1[:, tt, :], [B_x1[tt]], gff))
        for tt in range(4):
            norm_post(nslots[tt], hg, tt, B_hg[tt], 8)
        for (j0, n) in THIRDS:
            wk = next_wd()
            WDv, B_WD = WDs[wk], B_WDs[wk]
            P.dma('sp', WDv[:, 0:n, :], wd_bf[:, j0:j0 + n, :], reads=[B_wdbf], writes=[B_WD])
            for jj in range(n):
                j = j0 + jj
                k = tl['w'] % 3
                tl['w'] += 1
                P.dma('sp', wg[k], wg_bf[:, j, :, :], reads=[B_wgbf], writes=[B_wg[k]])
                P.dma('sp', wu[k], wu_bf[:, j, :, :], reads=[B_wubf], writes=[B_wu[k]])
                gb_, ub_ = j % 2, 2 + j % 2
                k2 = j % 2
                for c in range(8):
                    P.op('pe', (lambda gb_, c, k: lambda e: e.matmul(out=ps[gb_][:, :], lhsT=wg[k][:, c, :], rhs=hg[:, c, :],
                                                                    start=(c == 0), stop=(c == 7)))(gb_, c, k),
                         reads=[B_wg[k]] + B_hg, writes=[psb[gb_]])
                for c in range(8):
                    P.op('pe', (lambda ub_, c, k: lambda e: e.matmul(out=ps[ub_][:, :], lhsT=wu[k][:, c, :], rhs=hg[:, c, :],
                                                                    start=(c == 0), stop=(c == 7)))(ub_, c, k),
                         reads=[B_wu[k]] + B_hg, writes=[psb[ub_]])
                P.op('act', (lambda gb_, k2: lambda e: e.activation(out=gtmp[k2], in_=ps[gb_][:, :], func=AF.Silu))(gb_, k2),
                     reads=[psb[gb_]], writes=[B_gtmp[k2]])
                P.op('dve', (lambda ub_, k2, jj: lambda e: e.tensor_tensor(out=actT[:, jj, :], in0=ps[ub_][:, :], in1=gtmp[k2], op=ALU.mult))(ub_, k2, jj),
                     reads=[psb[ub_], B_gtmp[k2]], writes=[B_act[jj]])
            last_third = (j0 + n == 22)
            if last_third and g + 1 < NG:
                prefetch_group(g + 1)
            for tt in range(4):
                for half in range(2):
                    hc = slice(half * 512, (half + 1) * 512)
                    b = 4 + (tl['g'] % 4)
                    tl['g'] += 1
                    for jj in range(n):
                        P.op('pe', (lambda b, jj, tt, hc, n, WDv: lambda e: e.matmul(out=ps[b][:, :], lhsT=actT[:, jj, tt * 128:(tt + 1) * 128],
                                                                                    rhs=WDv[:, jj, hc], start=(jj == 0), stop=(jj == n - 1)))(b, jj, tt, hc, n, WDv),
                             reads=[B_act[jj], B_WD], writes=[psb[b]])
                    P.op('dve', (lambda b, tt, hc: lambda e: e.tensor_tensor(out=x1[:, tt, hc], in0=ps[b][:, :], in1=x1[:, tt, hc], op=ALU.add))(b, tt, hc),
                         reads=[psb[b], B_x1[tt]], writes=[B_x1[tt]])
                if last_third:
                    T = g * 4 + tt
                    col, bb = act_stats(x1[:, tt, :], [B_x1[tt]])
                    cols = [col]
                    P.op('dve', (lambda tt, cols: lambda e: e.scalar_tensor_tensor(out=x1[:, tt, :], in0=x1[:, tt, :], scalar=cols[0], in1=gfi,
                                                                                  op0=ALU.mult, op1=ALU.mult))(tt, cols),
                         reads=[B_x1[tt], bb, B_g3], writes=[B_x1[tt]])
                    P.dma('sp', out[T * 128:(T + 1) * 128, :], x1[:, tt, :], reads=[B_x1[tt]])
    P.barrier()

    P.finish()
    P.emit()
    return nc


def make_consts():
    c = np.zeros((128, 1024), np.float32)
    c[:, 0:128] = np.eye(128, dtype=np.float32)
    kk = np.arange(128)[:, None]
    qq = np.arange(256)[None, :]
    valid = (qq - kk >= 0) & (qq - kk <= 128)
    c[:, 128:384] = np.where(valid, 0.0, 1.0e6)
    c[0:64, 384:448] = np.eye(64, dtype=np.float32)
    c[0:64, 512 + 64:640] = np.eye(64, dtype=np.float32)
    c[64, 640:704] = 1.0
    return c


def prep_inputs(inputs):
    f = lambda a: np.ascontiguousarray(np.asarray(a, dtype=np.float32))
    w_in = f(inputs['w_in'])[0]
    kr = w_in[:, 2176:2240]
    kr_sw = np.concatenate([kr[:, 32:64], kr[:, 0:32]], axis=1)
    w_in_ext = np.concatenate([w_in, kr, kr_sw, kr_sw], axis=1)
    wq = f(inputs['w_q_up'])[0].reshape(384, 4, 192)
    wq_sw = np.concatenate([wq[:, :, 160:192], wq[:, :, 128:160]], axis=2)
    wq_ext = np.concatenate([wq, wq[:, :, 128:192], wq_sw, wq_sw], axis=2).reshape(384, 1536)
    pm = lambda w, c: np.ascontiguousarray(w.reshape(c, 128, -1).transpose(1, 0, 2))
    pj = lambda w: np.ascontiguousarray(w.reshape(8, 128, 22, 128).transpose(1, 2, 0, 3))
    gcols = np.zeros((128, 16), np.float32)
    gcols[:, 0:3] = f(inputs['q_norm'])[0].reshape(3, 128).T
    gcols[:, 3:5] = f(inputs['kv_norm'])[0].reshape(2, 128).T
    gcols[:, 5:9] = f(inputs['gout_a'])[0].reshape(4, 128).T
    gcols[:, 9:13] = f(inputs['gout_b'])[0].reshape(4, 128).T
    half = 32
    inv_freq = (np.float32(10000.0) ** (-np.arange(half, dtype=np.float32) / np.float32(half))).astype(np.float32)
    gcols[:, 13] = np.tile(inv_freq, 4)
    gcols[:, 14] = np.tile(np.concatenate([-np.ones(32, np.float32), np.ones(32, np.float32)]), 2)
    gcols2 = np.zeros((128, 16), np.float32)
    gcols2[:, 0:8] = f(inputs['norm_mem_q'])[0].reshape(8, 128).T
    gcols2[:, 8:16] = f(inputs['norm_ffn'])[0].reshape(8, 128).T
    shared = {
        'cst': make_consts(), 'gcols': gcols, 'gcols2': gcols2,
        'g_mix': f(inputs['norm_mix']).reshape(1, D), 'g_memq': f(inputs['norm_mem_q']).reshape(1, D),
        'g_memkv': f(inputs['norm_mem_kv']).reshape(1, D), 'g_ffn': f(inputs['norm_ffn']).reshape(1, D),
        'g_fin': f(inputs['norm_final']).reshape(1, D),
        'w_in': pm(w_in_ext, 8), 'w_qup': pm(wq_ext, 3), 'w_kvup': pm(f(inputs['w_kv_up'])[0], 2),
        'w_out': pm(f(inputs['w_out'])[0], 8), 'w_mq': pm(f(inputs['w_mq'])[0], 8),
        'w_mkv': pm(f(inputs['w_mkv'])[0], 8), 'w_mo': pm(f(inputs['w_mo'])[0], 4),
        'w_gate': pj(f(inputs['w_gate'])[0]), 'w_up': pj(f(inputs['w_up'])[0]),
        'w_down': pm(f(inputs['w_down'])[0], 22),
    }
    xs = f(inputs['x'])
    mems = f(inputs['mem'])
    poss = np.ascontiguousarray(np.asarray(inputs['positions'], dtype=np.int32))
    in_maps = []
    for b in range(NCORES):
        p = poss[b]
        pk = np.zeros((128, 96), np.int32)
        for L in range(3):
            for j in range(32):
                st, step = layout_tokens(L, j)
                pk[:, L * 32 + j] = p[st:st + 128 * step:step]
        m = dict(shared)
        m['x'] = xs[b]
        m['mem'] = mems[b]
        m['pos'] = p.reshape(1, S)
        m['posk'] = pk
        in_maps.append(m)
    return in_maps


_CACHE = {}


def kernel(**inputs):
    in_maps = prep_inputs(inputs)
    if 'nc' not in _CACHE:
        _CACHE['nc'] = build()
    res = run_bass_kernel_spmd(_CACHE['nc'], in_maps, core_ids=list(range(NCORES)))
    return np.stack([np.asarray(r['out'], dtype=np.float32) for r in res.results], axis=0)
```

```python
import numpy as np
import ml_dtypes
import concourse.bass as bass
import concourse.mybir as mybir
from concourse.bass_utils import run_bass_kernel_spmd

F32 = mybir.dt.float32
BF16 = mybir.dt.bfloat16
I32 = mybir.dt.int32
U8 = mybir.dt.uint8
ALU = mybir.AluOpType
AF = mybir.ActivationFunctionType

S = 4096
D = 1024
NCORES = 8
EPS = 1e-6
NMEM = 256
DFF = 2816
ENGS = ('pe', 'act', 'dve', 'pool', 'sp')
DTSIZE = {F32: 4, BF16: 2, I32: 4, U8: 1}


class Buf:
    __slots__ = ('name', 'w', 'r', 'rd', 'sem', 'tot')

    def __init__(self, name=''):
        self.name = name
        self.w = None
        self.r = {}
        self.rd = []
        self.sem = None
        self.tot = 0


class Op:
    __slots__ = ('eng', 'fn', 'waits', 'inc', 'idx', 'semval', 'snap')


class Prog:
    def __init__(self, nc):
        self.nc = nc
        self.ops = {e: [] for e in ENGS}
        self.seen = {e: {e2: -1 for e2 in ENGS} for e in ENGS}
        self.seen_dma = {e: {} for e in ENGS}
        self.esem = {e: nc.alloc_semaphore('es_' + e) for e in ENGS}
        self.last_real = {e: None for e in ENGS}
        self.pending = {}
        self.nsem = 0

    def _add(self, eng, fn, deps, real=True):
        o = Op()
        o.eng = eng
        o.fn = fn
        o.inc = False
        o.idx = len(self.ops[eng])
        o.waits = []
        seen = self.seen[eng]
        for tok in deps:
            if tok is None:
                continue
            if tok[0] == 'op':
                d = tok[1]
                if d.eng == eng and eng == 'pe':
                    continue
                if seen[d.eng] >= d.idx:
                    continue
                seen[d.eng] = d.idx
                for e3, v in d.snap.items():
                    if seen[e3] < v:
                        seen[e3] = v
                d.inc = True
                o.waits.append(('op', d))
            else:
                _, buf, val = tok
                val = max(val, buf.tot)
                sd = self.seen_dma[eng]
                if sd.get(id(buf), 0) >= val:
                    continue
                sd[id(buf)] = val
                o.waits.append(('dma', buf, val))
        o.snap = dict(seen)
        self.ops[eng].append(o)
        if fn is not None and real:
            self.last_real[eng] = o
        return o

    @staticmethod
    def _deps(reads, writes):
        deps = []
        for b in reads:
            deps.append(b.w)
        for b in writes:
            deps.append(b.w)
            deps.extend(b.r.values())
            deps.extend(b.rd)
        return deps

    def op(self, eng, fn, reads=(), writes=()):
        o = self._add(eng, fn, self._deps(reads, writes))
        tok = ('op', o)
        for b in reads:
            b.r[eng] = tok
        for b in writes:
            b.w = tok
            b.r = {}
            b.rd = []
        return o

    def dma(self, q, out_ap, in_ap, reads=(), writes=(), sembuf=None, **kw):
        deps = self._deps(reads, writes)
        sb = sembuf if sembuf is not None else (writes[0] if writes else reads[0])
        if sb.sem is None:
            sb.sem = self.nc.alloc_semaphore('ds%d' % self.nsem)
            self.nsem += 1
        o = self._add(q, None, deps, real=False)
        sb.tot += 16
        val = sb.tot
        sem = sb.sem
        o.fn = lambda e: e.dma_start(out=out_ap, in_=in_ap, **kw).then_inc(sem, 16)
        tok = ('dma', sb, val)
        for b in reads:
            b.rd.append(tok)
        for b in writes:
            b.w = tok
            b.r = {}
            b.rd = []
        self.pending[id(sb)] = sb
        return tok

    def barrier(self):
        deps = [('dma', sb, sb.tot) for sb in self.pending.values()]
        self._add('sp', lambda e: e.nop(), deps)
        lasts = {e: self.last_real[e] for e in ENGS if self.last_real[e] is not None}
        for e in ENGS:
            self._add(e, None, [('op', o) for e2, o in lasts.items()])

    def finish(self):
        deps = [('dma', sb, sb.tot) for sb in self.pending.values()]
        self._add('sp', lambda e: e.nop(), deps)

    def emit(self):
        nc = self.nc
        for e in ENGS:
            c = 0
            for o in self.ops[e]:
                if o.inc:
                    c += 1
                    o.semval = c
        esem = self.esem

        def mk(ename):
            def body(eng):
                for o in self.ops[ename]:
                    for w in o.waits:
                        if w[0] == 'op':
                            eng.wait_ge(esem[w[1].eng], w[1].semval)
                        else:
                            eng.wait_ge(w[1].sem, w[2])
                    if o.fn is not None:
                        ins = o.fn(eng)
                        if o.inc:
                            ins.then_inc(esem[ename], 1)
            return body

        with nc.Block() as block:
            block.sync(mk('sp'))
            block.scalar(mk('act'))
            block.vector(mk('dve'))
            block.gpsimd(mk('pool'))
            block.tensor(mk('pe'))


class Arena:
    def __init__(self, nc, nbytes):
        self.nbytes = nbytes
        self.t = nc.alloc_sbuf_tensor("arena", [128, nbytes], U8)

    def view(self, off, shape, dtype, p0=0):
        n = 1
        for s_ in shape[1:]:
            n *= s_
        nb = n * DTSIZE[dtype]
        assert off % 4 == 0 and off + nb <= self.nbytes, (off, nb, self.nbytes)
        ap = self.t[p0:p0 + shape[0], off:off + nb].bitcast(dtype)
        if len(shape) == 3:
            ap = ap.rearrange("p (a b) -> p a b", a=shape[1])
        elif len(shape) == 4:
            ap = ap.rearrange("p (a b c) -> p a b c", a=shape[1], b=shape[2])
        return ap


def layout_tokens(L, j):
    if L == 0:
        return j * 128, 1
    if L == 1:
        r, i0 = j // 8, (j % 8) * 128
        return r + 4 * i0, 4
    r, i0 = j // 2, (j % 2) * 128
    return r + 16 * i0, 16


def nat_slice(L, r, lo, hi):
    d = (1, 4, 16)[L]
    return slice(r + d * lo, r + d * (hi - 1) + 1, d)


A_SLOPES = [2.0 ** (-(i + 1)) for i in range(8)]


def build(stage=99):
    nc = bass.Bass("TRN2", target_bir_lowering=False)
    P = Prog(nc)

    def din(name, shape, dt=F32):
        return nc.dram_tensor(name, list(shape), dt, kind="ExternalInput").ap()

    x = din("x", [S, D])
    mem = din("mem", [NMEM, D])
    pos = din("pos", [1, S], I32)
    posk_in = din("posk", [128, 96], I32)
    cst = din("cst", [128, 1024])
    gcols = din("gcols", [128, 16])
    gcols2 = din("gcols2", [128, 16])
    g_mix = din("g_mix", [1, D])
    g_memq = din("g_memq", [1, D])
    g_memkv = din("g_memkv", [1, D])
    g_ffn = din("g_ffn", [1, D])
    g_fin = din("g_fin", [1, D])
    w_in = din("w_in", [128, 8, 2432])
    w_qup = din("w_qup", [128, 3, 1536])
    w_kvup = din("w_kvup", [128, 2, 1024])
    w_out = din("w_out", [128, 8, 1024])
    w_mq = din("w_mq", [128, 8, 512])
    w_mkv = din("w_mkv", [128, 8, 1024])
    w_mo = din("w_mo", [128, 4, 1024])
    w_gate = din("w_gate", [128, 22, 8, 128])
    w_up = din("w_up", [128, 22, 8, 128])
    w_down = din("w_down", [128, 22, D])
    out = nc.dram_tensor("out", [S, D], F32, kind="ExternalOutput").ap()
    dbg = nc.dram_tensor("dbg", [128, 8, S], F32, kind="ExternalOutput").ap() if stage != 99 else None

    wg_bf = nc.dram_tensor("wg_bf", [128, 22, 8, 128], BF16).ap()
    wu_bf = nc.dram_tensor("wu_bf", [128, 22, 8, 128], BF16).ap()
    wd_bf = nc.dram_tensor("wd_bf", [128, 22, D], BF16).ap()
    wo_bf = nc.dram_tensor("wo_bf", [128, 8, D], BF16).ap()
    B_wgbf, B_wubf, B_wdbf, B_wobf = Buf("wgbf"), Buf("wubf"), Buf("wdbf"), Buf("wobf")
    AR = Arena(nc, 206 * 1024)
    ps = [nc.alloc_psum_tensor("ps%d" % i, [128, 512], F32) for i in range(8)]
    psb = [Buf("ps%d" % i) for i in range(8)]

    ident = AR.view(0, [128, 128], BF16)
    ones = AR.view(256, [128, 128], BF16)
    zero = AR.view(512, [128, 512], BF16)
    negsl = AR.view(1536, [128, 8, 128], BF16)
    maskb = AR.view(3584, [128, 256], BF16)
    Emat = AR.view(4096, [64, 2, 128], BF16)
    sel = AR.view(4608, [65, 64], F32)
    gc = AR.view(4864, [128, 16], F32)
    poskf = AR.view(4928, [128, 96], F32)
    small = AR.view(5312, [128, 64], F32)
    negposk = AR.view(7616, [128, 96], F32)
    gc2 = AR.view(8000, [128, 16], F32)
    B_const = Buf("const")
    B_small = [Buf("small%d" % i) for i in range(64)]
    posq = AR.view(8192, [128, S], F32)
    B_posq = Buf("posq")
    OAT = 24576
    OBT_P2 = 57344
    HT = 90112
    LOC = 155648
    oaT = AR.view(OAT, [128, 4, S], BF16)
    B_oaT = [[Buf("oaT%d_%d" % (c, q)) for q in range(8)] for c in range(4)]
    hT = AR.view(HT, [128, 8, S], BF16)
    B_hT = [Buf("hT%d" % q) for q in range(8)]

    cst_s = AR.view(LOC, [128, 1024], F32)
    posi = AR.view(LOC + 4096, [128, S], I32)
    poski = AR.view(LOC + 4096 + 16384, [128, 96], I32)
    B_cst = Buf("cst_s")
    B_posi = Buf("posi")
    B_poski = Buf("poski")
    P.dma('sp', cst_s, cst, writes=[B_cst])
    P.dma('sp', posi, pos.partition_broadcast(128)[:, 0, :] if False else bass.AP(pos.tensor, 0, [[0, 128], [1, S]]),
          writes=[B_posi])
    P.dma('sp', poski, posk_in, writes=[B_poski])
    B_gc = Buf("gc")
    P.dma('sp', gc, gcols, writes=[B_gc])
    P.dma('sp', gc2, gcols2, writes=[B_gc])
    P.op('dve', lambda e: e.tensor_copy(out=ident, in_=cst_s[:, 0:128]), reads=[B_cst], writes=[B_const])
    P.op('dve', lambda e: e.tensor_copy(out=maskb, in_=cst_s[:, 128:384]), reads=[B_cst], writes=[B_const])
    P.op('dve', lambda e: e.tensor_copy(out=Emat[:, 0, :], in_=cst_s[0:64, 384:512]), reads=[B_cst], writes=[B_const])
    P.op('dve', lambda e: e.tensor_copy(out=Emat[:, 1, :], in_=cst_s[0:64, 512:640]), reads=[B_cst], writes=[B_const])
    P.op('dve', lambda e: e.tensor_copy(out=sel, in_=cst_s[0:65, 640:704]), reads=[B_cst], writes=[B_const])
    P.op('dve', lambda e: e.memset(ones, 1.0), writes=[B_const])
    P.op('dve', lambda e: e.memset(zero, 0.0), writes=[B_const])
    for h in range(8):
        sc = -8.0 * A_SLOPES[h]
        P.op('dve', (lambda h, sc: lambda e: e.tensor_scalar(out=negsl[:, h, :], in0=cst_s[:, 0:128], scalar1=sc,
                                                            scalar2=None, op0=ALU.mult))(h, sc),
             reads=[B_cst], writes=[B_const])
    P.op('dve', lambda e: e.tensor_copy(out=posq, in_=posi), reads=[B_posi], writes=[B_posq])
    P.op('dve', lambda e: e.tensor_copy(out=poskf, in_=poski), reads=[B_poski], writes=[B_const])
    P.op('dve', lambda e: e.tensor_scalar(out=negposk, in0=poskf, scalar1=-1.0, scalar2=None, op0=ALU.mult),
         reads=[B_const], writes=[B_const])
    P.barrier()

    gmix = AR.view(LOC, [128, D], F32)
    B_gmix = Buf("gmix")
    P.dma('sp', gmix, bass.AP(g_mix.tensor, 0, [[0, 128], [1, D]]), writes=[B_gmix])
    xt = [AR.view(LOC + 4096 + i * 4096, [128, D], F32) for i in range(2)]
    B_xt = [Buf("xt%d" % i) for i in range(2)]
    xn = [AR.view(LOC + 12288 + i * 2048, [128, D], BF16) for i in range(2)]
    B_xn = [Buf("xn%d" % i) for i in range(2)]
    junk = AR.view(LOC + 16384, [128, D], F32)
    B_junk = Buf("junk")
    psT = [ps[i][:, :].bitcast(BF16).rearrange("p (a b) -> p a b", a=8) for i in range(8)]
    for t in range(32):
        sl = t % 2
        ssb = B_small[sl]
        ss = small[:, sl:sl + 1]
        rs = small[:, 2 + sl:3 + sl]
        rsb = B_small[2 + sl]
        P.dma('sp', xt[sl], x[t * 128:(t + 1) * 128, :], writes=[B_xt[sl]])
        P.op('dve', (lambda sl, ss: lambda e: e.scalar_tensor_tensor(
            out=junk, in0=xt[sl], scalar=1.0, in1=xt[sl], op0=ALU.mult, op1=ALU.mult, accum_out=ss))(sl, ss),
            reads=[B_xt[sl]], writes=[B_junk, ssb])
        P.op('act', (lambda ss, rs: lambda e: e.activation(out=rs, in_=ss, func=AF.Sqrt, bias=EPS, scale=1.0 / D))(ss, rs),
             reads=[ssb], writes=[rsb])
        P.op('dve', (lambda rs: lambda e: e.reciprocal(out=rs, in_=rs))(rs), reads=[rsb], writes=[rsb])
        P.op('dve', (lambda sl, rs: lambda e: e.scalar_tensor_tensor(
            out=xn[sl], in0=xt[sl], scalar=rs, in1=gmix, op0=ALU.mult, op1=ALU.mult))(sl, rs),
            reads=[B_xt[sl], rsb, B_gmix], writes=[B_xn[sl]])
        pb = t % 2
        for c in range(8):
            P.op('pe', (lambda sl, c, pb: lambda e: e.transpose(out=psT[pb][:, c, :], in_=xn[sl][:, c * 128:(c + 1) * 128],
                                                               identity=ident))(sl, c, pb),
                 reads=[B_xn[sl], B_const], writes=[psb[pb]])
        P.op('act', (lambda t, pb: lambda e: e.copy(out=hT[:, :, t * 128:(t + 1) * 128], in_=psT[pb]))(t, pb),
             reads=[psb[pb]], writes=[B_hT[t // 4]])
    P.barrier()

    if stage == 1:
        stg = AR.view(LOC, [128, S], F32)
        B_stg = Buf("stg")
        for c in range(8):
            P.op('dve', (lambda c: lambda e: e.tensor_copy(out=stg, in_=hT[:, c, :]))(c), reads=B_hT, writes=[B_stg])
            P.dma('sp', dbg[:, c, :], stg, reads=[B_stg])
        P.finish()
        P.emit()
        return nc


    wA = [AR.view(LOC + i * 6144, [128, 8, 3, 128], BF16) for i in range(2)]
    B_wA = [[Buf("wA%d_%d" % (i, w)) for w in range(3)] for i in range(2)]
    qkvT = [AR.view(LOC + 12288 + i * 8192, [128, S], BF16) for i in range(3)]
    B_qkv = [Buf("qkvT%d" % i) for i in range(3)]
    Vaug = AR.view(LOC + 36864, [128, 32, 2, 65], BF16)
    B_Vaug = Buf("Vaug")
    dts = [AR.view(LOC + 45312 + i * 512, [128, 256], BF16) for i in range(4)]
    B_dt = [Buf("dt%d" % i) for i in range(4)]
    dabs = [AR.view(5568 + i * 512, [128, 256], BF16) for i in range(4)]
    B_dabs = [Buf("dabs%d" % i) for i in range(4)]
    PTs = [AR.view(LOC + 47360 + i * 512, [128, 256], BF16) for i in range(4)]
    B_PT = [Buf("PT%d" % i) for i in range(4)]
    on = [AR.view(LOC + 49408 + i * 1024, [64, 512], BF16) for i in range(2)]
    B_on = [Buf("on%d" % i) for i in range(2)]
    rec = AR.view(LOC + 51456, [64, 512], F32)
    B_rec = Buf("rec")
    accn = [AR.view(OBT_P2 + i * 16384, [65, S], F32) for i in range(2)]
    B_accn = [Buf("accn%d" % i) for i in range(2)]

    P.op('dve', lambda e: e.memset(Vaug[:, :, :, 64:65], 1.0), writes=[B_Vaug])

    def load_wA(p):
        sl = p % 2
        for w in range(3):
            off = w * 512 + p * 128
            P.dma('pool', wA[sl][:, :, w, :], w_in[:, :, off:off + 128], writes=[B_wA[sl][w]])

    cnt = {'ev': 0, 's': 0, 'dt': 0, 'pt': 0, 'acc': [0, 0]}
    rec2 = AR.view(LOC + 45312, [64, 512], F32)
    recs = [(rec, [B_rec]), (rec2, B_dt)]

    def proj_groups(p):
        sl = p % 2
        out_ = []
        for w in range(3):
            for qb in range(8):
                def grp(w=w, qb=qb, sl=sl):
                    b = qb % 2
                    for c in range(8):
                        P.op('pe', (lambda c: lambda e: e.matmul(
                            out=ps[b][:, :], lhsT=wA[sl][:, c, w, :], rhs=hT[:, c, qb * 512:(qb + 1) * 512],
                            start=(c == 0), stop=(c == 7)))(c),
                            reads=[B_wA[sl][w], B_hT[qb]], writes=[psb[b]])
                    if cnt['ev'] % 2 == 0:
                        P.op('act', lambda e: e.copy(out=qkvT[w][:, qb * 512:(qb + 1) * 512], in_=ps[b][:, :]),
                             reads=[psb[b]], writes=[B_qkv[w]])
                    else:
                        P.op('dve', lambda e: e.tensor_copy(out=qkvT[w][:, qb * 512:(qb + 1) * 512], in_=ps[b][:, :]),
                             reads=[psb[b]], writes=[B_qkv[w]])
                    cnt['ev'] += 1
                out_.append(grp)
        return out_

    def finalize_steps(p):
        f1, f2 = [], []
        for blk in range(8):
            for hh in range(2):
                def s1(blk=blk, hh=hh):
                    cs = slice(blk * 512, (blk + 1) * 512)
                    si = (blk * 2 + hh) % 2
                    sbk = 2 + si
                    rc, brc = recs[si]
                    P.op('pe', lambda e: e.matmul(out=ps[sbk][0:64, :], lhsT=sel[:, :], rhs=accn[hh][:, cs], start=True, stop=True),
                         reads=[B_const, B_accn[hh]], writes=[psb[sbk]])
                    P.op('act', lambda e: e.activation(out=rc[:, :], in_=ps[sbk][0:64, :], func=AF.Ln), reads=[psb[sbk]], writes=brc)
                    P.op('act', lambda e: e.activation(out=rc[:, :], in_=rc[:, :], func=AF.Exp, scale=-1.0), reads=brc, writes=brc)
                    P.op('dve', lambda e: e.tensor_tensor(out=on[hh][:, :], in0=accn[hh][0:64, cs], in1=rc[:, :], op=ALU.mult),
                         reads=[B_accn[hh]] + brc, writes=[B_on[hh]])

                def s2(blk=blk, hh=hh, p=p):
                    cs = slice(blk * 512, (blk + 1) * 512)
                    cb = 4 + blk % 2
                    P.op('pe', lambda e: e.matmul(out=ps[cb][:, :], lhsT=Emat[:, hh, :], rhs=on[hh][:, :], start=(hh == 0), stop=(hh == 1)),
                         reads=[B_const, B_on[hh]], writes=[psb[cb]])
                    if hh == 1:
                        P.op('act', lambda e: e.copy(out=oaT[:, p, cs], in_=ps[cb][:, :]), reads=[psb[cb]], writes=[B_oaT[p][blk]])
                f1.append(s1)
                f2.append(s2)
        return f1, f2

    load_wA(0)
    for dst_, src_, bb_ in ((wg_bf, w_gate, B_wgbf), (wu_bf, w_up, B_wubf)):
        P.dma('pool', dst_.rearrange("p j c f -> (p j) (c f)"), src_.rearrange("p j c f -> (p j) (c f)"), writes=[bb_])
    P.dma('pool', wd_bf.rearrange("p j f -> (p j) f"), w_down.rearrange("p j f -> (p j) f"), writes=[B_wdbf])
    P.dma('pool', wo_bf.rearrange("p j f -> (p j) f"), w_out.rearrange("p j f -> (p j) f"), writes=[B_wobf])
    for bb_ in (B_wgbf, B_wubf, B_wdbf, B_wobf):
        P.pending.pop(id(bb_))
    NPAIR = 4 if stage not in (21, 31) else 1
    for p in range(NPAIR):
        sl = p % 2
        if p + 1 < NPAIR:
            load_wA(p + 1)
        if p == 0:
            for pg in proj_groups(0):
                pg()
        for L in range(3):
            dil = (1, 4, 16)[L]
            Lc = S // dil
            QB = min(512, Lc)
            for j4 in range(8):
                b = 2 + (j4 % 2)
                for jj in range(4):
                    j = j4 * 4 + jj
                    st, step = layout_tokens(L, j)
                    P.op('pe', (lambda b, jj, st, step: lambda e: e.transpose(
                        out=psT[b][:, jj, :], in_=qkvT[2][:, st:st + 127 * step + 1:step], identity=ident))(b, jj, st, step),
                        reads=[B_qkv[2], B_const], writes=[psb[b]])
                P.op('dve', (lambda b, j4: lambda e: e.tensor_copy(
                    out=Vaug[:, j4 * 4:(j4 + 1) * 4, :, 0:64],
                    in_=psT[b][:, 0:4, :].rearrange("p a (h d) -> p a h d", h=2)))(b, j4),
                    reads=[psb[b]], writes=[B_Vaug])
            items = []
            for r in range(dil):
                for Q0 in range(0, Lc, QB):
                    k0s = [k0 for k0 in range(Q0 - 128, Q0 + QB + 1, 128) if 0 <= k0 < Lc]
                    for ki, k0 in enumerate(k0s):
                        items.append((r, Q0, k0, ki == 0, ki == len(k0s) - 1))

            dist_of = {}

            def emit_dist(it, L=L, QB=QB):
                r, Q0, k0, first, last = it
                qlo = max(Q0, k0 - 64)
                qhi = min(Q0 + QB, k0 + 192)
                n = qhi - qlo
                col = L * 32 + (0, r * 8, r * 2)[L] + k0 // 128
                m0 = qlo - (k0 - 64)
                di = cnt['dt'] % 4
                cnt['dt'] += 1
                dist_of[it] = di
                qs = nat_slice(L, r, qlo, qhi)
                if cnt['dt'] % 2 == 0:
                    P.op('act', lambda e: e.activation(out=dabs[di][:, 0:n], in_=posq[:, qs], func=AF.Abs,
                                                       bias=negposk[:, col:col + 1], scale=1.0),
                         reads=[B_posq, B_const], writes=[B_dabs[di]])
                    P.op('dve', lambda e: e.tensor_tensor(out=dts[di][:, 0:n], in0=dabs[di][:, 0:n], in1=maskb[:, m0:m0 + n], op=ALU.max),
                         reads=[B_dabs[di], B_const], writes=[B_dt[di]])
                else:
                    P.op('dve', lambda e: e.tensor_scalar(out=dabs[di][:, 0:n], in0=posq[:, qs], scalar1=negposk[:, col:col + 1],
                                                          scalar2=None, op0=ALU.add),
                         reads=[B_posq, B_const], writes=[B_dabs[di]])
                    P.op('dve', lambda e: e.scalar_tensor_tensor(out=dts[di][:, 0:n], in0=dabs[di][:, 0:n], scalar=-1.0,
                                                                 in1=dabs[di][:, 0:n], op0=ALU.mult, op1=ALU.max),
                         reads=[B_dabs[di]], writes=[B_dt[di]])
                    P.op('dve', lambda e: e.tensor_tensor(out=dts[di][:, 0:n], in0=dts[di][:, 0:n], in1=maskb[:, m0:m0 + n], op=ALU.max),
                         reads=[B_dt[di], B_const], writes=[B_dt[di]])

            def emit_scores(it, L=L, QB=QB, p=p):
                r, Q0, k0, first, last = it
                qlo = max(Q0, k0 - 64)
                qhi = min(Q0 + QB, k0 + 192)
                n = qhi - qlo
                j = (0, r * 8, r * 2)[L] + k0 // 128
                col = L * 32 + j
                m0 = qlo - (k0 - 64)
                di = dist_of[it]
                qs = nat_slice(L, r, qlo, qhi)
                ks = nat_slice(L, r, k0, k0 + 128)
                pis, sbs = [], []
                for hh in range(2):
                    sbs.append(cnt['s'] % 4)
                    cnt['s'] += 1
                    pis.append(cnt['pt'] % 4)
                    cnt['pt'] += 1
                for hh in range(2):
                    sb = sbs[hh]
                    lo, hi = hh * 64, hh * 64 + 64
                    P.op('pe', (lambda sb, lo, hi: lambda e: e.matmul(
                        out=ps[sb][:, 0:n], lhsT=qkvT[1][lo:hi, ks], rhs=qkvT[0][lo:hi, qs], start=True, stop=False))(sb, lo, hi),
                        reads=[B_qkv[0], B_qkv[1]], writes=[psb[sb]])
                for hh in range(2):
                    sb, head = sbs[hh], 2 * p + hh
                    P.op('pe', (lambda sb, head: lambda e: e.matmul(
                        out=ps[sb][:, 0:n], lhsT=negsl[:, head, :], rhs=dts[di][:, 0:n], start=False, stop=True))(sb, head),
                        reads=[B_const, B_dt[di]], writes=[psb[sb]])
                for hh in range(2):
                    sb, pi = sbs[hh], pis[hh]
                    P.op('act', (lambda sb, pi: lambda e: e.activation(
                        out=PTs[pi][:, 0:n], in_=ps[sb][:, 0:n], func=AF.Exp, scale=0.125))(sb, pi),
                        reads=[psb[sb]], writes=[B_PT[pi]])
                return (qlo, qhi, n, j, pis)

            def emit_pv(it, sc, L=L, QB=QB):
                r, Q0, k0, first, last = it
                qlo, qhi, n, j, pis = sc
                if first:
                    accb = []
                    for hh in range(2):
                        ab = 4 + 2 * hh + (cnt['acc'][hh] % 2)
                        cnt['acc'][hh] += 1
                        accb.append(ab)
                        P.op('pe', (lambda ab: lambda e: e.matmul(
                            out=ps[ab][0:65, 0:QB], lhsT=zero[:, 0:65], rhs=zero[:, 0:QB], start=True, stop=False))(ab),
                            reads=[B_const], writes=[psb[ab]])
                    cnt['accb'] = accb
                accb = cnt['accb']
                for hh in range(2):
                    ab = accb[hh]
                    pi = pis[hh]
                    P.op('pe', (lambda ab, hh, pi: lambda e: e.matmul(
                        out=ps[ab][0:65, qlo - Q0:qhi - Q0], lhsT=Vaug[:, j, hh, :], rhs=PTs[pi][:, 0:n],
                        start=False, stop=False, skip_group_check=True))(ab, hh, pi),
                        reads=[B_Vaug, B_PT[pi]], writes=[psb[ab]])
                if last:
                    ns = nat_slice(L, r, Q0, Q0 + QB)
                    for hh in range(2):
                        ab = accb[hh]
                        P.op('pe', (lambda ab: lambda e: e.matmul(
                            out=ps[ab][0:65, 0:QB], lhsT=zero[:, 0:65], rhs=zero[:, 0:QB], start=False, stop=True))(ab),
                            reads=[B_const], writes=[psb[ab]])
                        if L == 0:
                            P.op('dve', (lambda hh, ab: lambda e: e.tensor_copy(
                                out=accn[hh][:, ns], in_=ps[ab][0:65, 0:QB]))(hh, ab),
                                reads=[psb[ab]], writes=[B_accn[hh]])
                        else:
                            P.op('dve', (lambda hh, ab: lambda e: e.tensor_tensor(
                                out=accn[hh][:, ns], in0=accn[hh][:, ns], in1=ps[ab][0:65, 0:QB], op=ALU.add))(hh, ab),
                                reads=[psb[ab], B_accn[hh]], writes=[B_accn[hh]])

            prev = None
            emit_dist(items[0])
            for ii, it in enumerate(items):
                if ii + 1 < len(items):
                    emit_dist(items[ii + 1])
                sc = emit_scores(it)
                if prev is not None:
                    emit_pv(*prev)
                prev = (it, sc)
            emit_pv(*prev)
        f1, f2 = finalize_steps(p)
        pgs = proj_groups(p + 1) if p + 1 < NPAIR else []
        nst = len(f1)
        for i in range(max(len(pgs), nst + 1)):
            if i < len(pgs):
                pgs[i]()
            if i >= 1 and i - 1 < nst:
                f2[i - 1]()
            if i < nst:
                f1[i]()
    P.barrier()

    if stage in (2, 21):
        stg = AR.view(LOC, [128, S], F32)
        B_stg = Buf("stg")
        for c in range(4):
            P.op('dve', (lambda c: lambda e: e.tensor_copy(out=stg, in_=oaT[:, c, :]))(c), reads=B_oaT[c], writes=[B_stg])
            P.dma('sp', dbg[:, c, :], stg, reads=[B_stg])
        P.finish()
        P.emit()
        return nc


    PI = 3.141592653589793
    cosT = AR.view(LOC, [128, S], F32)
    sinT = AR.view(LOC + 16384, [128, S], F32)
    B_cos = [Buf("cos%d" % i) for i in range(8)]
    B_sin = [Buf("sin%d" % i) for i in range(8)]
    WB0 = 188416
    wB = AR.view(WB0, [128, 8, 896], BF16)
    B_wB = Buf("wB")
    sqs = [AR.view(202752 + i * 1024, [128, 512], BF16) for i in range(2)]
    B_sq = [Buf("sq%d" % i) for i in range(2)]
    rstd = AR.view(204800, [128, 512], F32)
    B_rstd = Buf("rstd")
    TB_ = AR.view(206848, [128, 512], F32)
    TA_ = AR.view(208896, [128, 512], F32)
    TI_ = AR.view(81920, [128, 512], I32)
    B_TA, B_TB, B_TI = Buf("TA"), Buf("TB"), Buf("TI")
    invf = gc[:, 13:14]
    sgn = gc[:, 14:15]
    P.dma('pool', wB, w_in[:, :, 1536:2432], writes=[B_wB])
    for blk in range(8):
        cs = slice(blk * 512, (blk + 1) * 512)
        P.op('dve', (lambda cs: lambda e: e.tensor_scalar(out=TA_, in0=posq[:, cs], scalar1=invf, scalar2=None, op0=ALU.mult))(cs),
             reads=[B_posq, B_gc], writes=[B_TA])
        P.op('dve', lambda e: e.tensor_scalar(out=TI_, in0=TA_, scalar1=1.0 / (2 * PI), scalar2=None, op0=ALU.mult),
             reads=[B_TA], writes=[B_TI])
        P.op('dve', lambda e: e.tensor_copy(out=TB_, in_=TI_), reads=[B_TI], writes=[B_TB])
        P.op('dve', lambda e: e.scalar_tensor_tensor(out=TA_, in0=TB_, scalar=-2 * PI, in1=TA_, op0=ALU.mult, op1=ALU.add),
             reads=[B_TB, B_TA], writes=[B_TA])
        P.op('dve', lambda e: e.tensor_scalar(out=TB_, in0=TA_, scalar1=PI, scalar2=-2 * PI, op0=ALU.is_gt, op1=ALU.mult),
             reads=[B_TA], writes=[B_TB])
        P.op('dve', lambda e: e.tensor_tensor(out=TA_, in0=TA_, in1=TB_, op=ALU.add), reads=[B_TA, B_TB], writes=[B_TA])
        P.op('dve', lambda e: e.tensor_scalar(out=TB_, in0=TA_, scalar1=-PI, scalar2=2 * PI, op0=ALU.is_lt, op1=ALU.mult),
             reads=[B_TA], writes=[B_TB])
        P.op('dve', lambda e: e.tensor_tensor(out=TA_, in0=TA_, in1=TB_, op=ALU.add), reads=[B_TA, B_TB], writes=[B_TA])
        P.op('dve', lambda e: e.tensor_scalar(out=TA_, in0=TA_, scalar1=3.14159, scalar2=-3.14159, op0=ALU.min, op1=ALU.max),
             reads=[B_TA], writes=[B_TA])
        P.op('act', (lambda cs: lambda e: e.activation(out=sinT[:, cs], in_=TA_, func=AF.Sin, scale=sgn))(cs),
             reads=[B_TA, B_gc], writes=[B_sin[blk]])
        P.op('act', lambda e: e.activation(out=TB_, in_=TA_, func=AF.Abs), reads=[B_TA], writes=[B_TB])
        P.op('act', (lambda cs: lambda e: e.activation(out=cosT[:, cs], in_=TB_, func=AF.Sin, scale=-1.0, bias=PI / 2))(cs),
             reads=[B_TB], writes=[B_cos[blk]])

    ckvnT = AR.view(8192, [128, 2, S], BF16)
    cqnT = AR.view(57344, [128, 3, S], BF16)
    kpeT = AR.view(81920, [128, S], BF16)
    B_cqn = [Buf("cqn%d" % i) for i in range(8)]
    B_ckvn = [Buf("ckvn%d" % i) for i in range(8)]
    B_kpe = [Buf("kpe%d" % i) for i in range(8)]

    def rope(qb, psa, psb_a, psc, psb_c, dst, B_dst):
        cs = slice(qb * 512, (qb + 1) * 512)
        P.op('dve', (lambda cs: lambda e: e.tensor_tensor(out=TA_, in0=psa, in1=cosT[:, cs], op=ALU.mult))(cs),
             reads=[psb_a, B_cos[qb]], writes=[B_TA])
        P.op('dve', (lambda cs: lambda e: e.tensor_tensor(out=TB_, in0=psc, in1=sinT[:, cs], op=ALU.mult))(cs),
             reads=[psb_c, B_sin[qb]], writes=[B_TB])
        P.op('dve', (lambda cs: lambda e: e.tensor_tensor(out=dst[:, cs], in0=TA_, in1=TB_, op=ALU.add))(cs),
             reads=[B_TA, B_TB], writes=[B_dst] + ([B_TI] if dst is kpeT else []))

    rr = {'b': 0, 'sq': 0, 'ev': 0}
    RAWB = (0, 1, 6, 7)

    def evac(dst_ap, src_ap, reads, writes):
        if rr['ev'] % 2 == 0:
            P.op('act', lambda e: e.copy(out=dst_ap, in_=src_ap), reads=reads, writes=writes)
        else:
            P.op('dve', lambda e: e.tensor_copy(out=dst_ap, in_=src_ap), reads=reads, writes=writes)
        rr['ev'] += 1

    deferred = []

    def flush():
        for f_ in deferred:
            f_()
        del deferred[:]

    for qb in range(8):
        cs = slice(qb * 512, (qb + 1) * 512)
        for (bank, col0) in ((4, 640), (5, 768)):
            for c in range(8):
                P.op('pe', (lambda bank, c, col0, cs: lambda e: e.matmul(
                    out=ps[bank][:, :], lhsT=wB[:, c, col0:col0 + 128], rhs=hT[:, c, cs],
                    start=(c == 0), stop=(c == 7)))(bank, c, col0, cs),
                    reads=[B_wB, B_hT[qb]], writes=[psb[bank]])
        flush()
        rope(qb, ps[4][:, :], psb[4], ps[5][:, :], psb[5], kpeT, B_kpe[qb])
        for (dst, Bd, nch, col0, gcol0, sbank) in ((cqnT, B_cqn, 3, 0, 0, 2), (ckvnT, B_ckvn, 2, 384, 3, 3)):
            for i in range(nch):
                b = RAWB[rr['b'] % 4]
                rr['b'] += 1
                for c in range(8):
                    P.op('pe', (lambda b, c, i, col0, cs: lambda e: e.matmul(
                        out=ps[b][:, :], lhsT=wB[:, c, col0 + i * 128:col0 + (i + 1) * 128], rhs=hT[:, c, cs],
                        start=(c == 0), stop=(c == 7)))(b, c, i, col0, cs),
                        reads=[B_wB, B_hT[qb]], writes=[psb[b]])
                flush()
                P.op('act', (lambda dst, i, cs, b: lambda e: e.copy(out=dst[:, i, cs], in_=ps[b][:, :]))(dst, i, cs, b),
                     reads=[psb[b]], writes=[Bd[qb]] + ([B_posq] if dst is ckvnT else []))
                si = rr['sq'] % 2
                rr['sq'] += 1
                P.op('act', (lambda si, b: lambda e: e.activation(out=sqs[si], in_=ps[b][:, :], func=AF.Square))(si, b),
                     reads=[psb[b]], writes=[B_sq[si]])

                def ones_mm(sbank=sbank, si=si, i=i, nch=nch):
                    P.op('pe', lambda e: e.matmul(out=ps[sbank][:, :], lhsT=ones, rhs=sqs[si], start=(i == 0), stop=(i == nch - 1)),
                         reads=[B_const, B_sq[si]], writes=[psb[sbank]])
                deferred.append(ones_mm)

            def norm_tail(dst=dst, Bd=Bd, nch=nch, gcol0=gcol0, sbank=sbank, qb=qb, cs=cs):
                nf = nch * 128
                P.op('act', lambda e: e.activation(out=rstd, in_=ps[sbank][:, :], func=AF.Ln, bias=EPS, scale=1.0 / nf),
                     reads=[psb[sbank]], writes=[B_rstd])
                P.op('act', lambda e: e.activation(out=rstd, in_=rstd, func=AF.Exp, scale=-0.5), reads=[B_rstd], writes=[B_rstd])
                for i in range(nch):
                    P.op('dve', (lambda i: lambda e: e.scalar_tensor_tensor(
                        out=dst[:, i, cs], in0=dst[:, i, cs], scalar=gc[:, gcol0 + i:gcol0 + i + 1], in1=rstd,
                        op0=ALU.mult, op1=ALU.mult))(i),
                        reads=[Bd[qb], B_gc, B_rstd], writes=[Bd[qb]])
            deferred.append(norm_tail)
    flush()
    P.barrier()

    OBT = 90112
    obT = AR.view(OBT, [128, 4, S], BF16)
    B_obT = [[Buf("obT%d_%d" % (c, q)) for q in range(8)] for c in range(4)]
    qnT = AR.view(122880, [128, S], BF16)
    qpeT = AR.view(131072, [128, S], BF16)
    knT = AR.view(139264, [128, S], BF16)
    vtok = AR.view(147456, [128, 32, 128], BF16)
    B_qn = [Buf("qn%d" % i) for i in range(8)]
    B_qpe = [Buf("qpe%d" % i) for i in range(8)]
    B_kn = [Buf("kn%d" % i) for i in range(8)]
    B_vt = [Buf("vt%d" % i) for i in range(8)]
    wq = [AR.view(WB0 + i * 3328, [128, 3, 384], BF16) for i in range(2)]
    wkv = [AR.view(WB0 + i * 3328 + 2304, [128, 2, 256], BF16) for i in range(2)]
    B_wq = [Buf("wq%d" % i) for i in range(2)]
    B_wkv = [Buf("wkv%d" % i) for i in range(2)]
    PT5 = [AR.view(WB0 + 6656 + i * 1024, [128, 512], BF16) for i in range(4)]
    B_PT5 = [Buf("PT5_%d" % i) for i in range(4)]
    rec5 = AR.view(WB0 + 10752, [128, 512], F32)
    B_rec5 = Buf("rec5")
    accD = [AR.view(WB0 + 12800 + i * 2048, [128, 512], F32) for i in range(2)]
    B_accD = [Buf("accD%d" % i) for i in range(2)]
    ones32 = AR.view(WB0 + 16896, [128, 128], F32)
    P.op('dve', lambda e: e.memset(ones32, 1.0), writes=[B_const])

    def load_wh(h):
        sl = h % 2
        P.dma('pool', wq[sl], w_qup[:, :, h * 384:(h + 1) * 384], writes=[B_wq[sl]])
        P.dma('pool', wkv[sl], w_kvup[:, :, h * 256:(h + 1) * 256], writes=[B_wkv[sl]])

    NH = 4 if stage != 31 else 1
    load_wh(0)
    SC_B = 192.0 ** -0.5
    cnt5 = {'s': 0, 'pt': 0}
    for h in range(NH):
        sl = h % 2
        if h + 1 < NH:
            load_wh(h + 1)
        for qb in range(8):
            cs = slice(qb * 512, (qb + 1) * 512)
            b = rr['b'] % 4
            rr['b'] += 1
            for c in range(3):
                P.op('pe', (lambda b, c, cs, sl: lambda e: e.matmul(out=ps[b][:, :], lhsT=wq[sl][:, c, 0:128], rhs=cqnT[:, c, cs],
                                                                   start=(c == 0), stop=(c == 2)))(b, c, cs, sl),
                     reads=[B_wq[sl], B_cqn[qb]], writes=[psb[b]])
            evac(qnT[:, cs], ps[b][:, :], [psb[b]], [B_qn[qb]])
            rb = 4 + 2 * (qb % 2)
            for (bank, col0) in ((rb, 128), (rb + 1, 256)):
                for c in range(3):
                    P.op('pe', (lambda bank, c, col0, cs, sl: lambda e: e.matmul(
                        out=ps[bank][:, :], lhsT=wq[sl][:, c, col0:col0 + 128], rhs=cqnT[:, c, cs],
                        start=(c == 0), stop=(c == 2)))(bank, c, col0, cs, sl),
                        reads=[B_wq[sl], B_cqn[qb]], writes=[psb[bank]])
            rope(qb, ps[rb][:, :], psb[rb], ps[rb + 1][:, :], psb[rb + 1], qpeT, B_qpe[qb])
            b = rr['b'] % 4
            rr['b'] += 1
            for c in range(2):
                P.op('pe', (lambda b, c, cs, sl: lambda e: e.matmul(out=ps[b][:, :], lhsT=wkv[sl][:, c, 0:128], rhs=ckvnT[:, c, cs],
                                                                   start=(c == 0), stop=(c == 1)))(b, c, cs, sl),
                     reads=[B_wkv[sl], B_ckvn[qb]], writes=[psb[b]])
            evac(knT[:, cs], ps[b][:, :], [psb[b]], [B_kn[qb]])
            b = rr['b'] % 4
            rr['b'] += 1
            for tt in range(4):
                ts_ = slice(qb * 512 + tt * 128, qb * 512 + (tt + 1) * 128)
                for c in range(2):
                    P.op('pe', (lambda b, c, tt, ts_, sl: lambda e: e.matmul(
                        out=ps[b][:, tt * 128:(tt + 1) * 128], lhsT=ckvnT[:, c, ts_], rhs=wkv[sl][:, c, 128:256],
                        start=(c == 0), stop=(c == 1)))(b, c, tt, ts_, sl),
                        reads=[B_wkv[sl], B_ckvn[qb]], writes=[psb[b]])
            evac(vtok[:, qb * 4:(qb + 1) * 4, :], ps[b][:, :].rearrange("p (a b) -> p a b", a=4), [psb[b]], [B_vt[qb]])
        units = [(qb, kp) for qb in range(8) for kp in range(16)]

        def emit_S(u, h=h):
            qb, kp = u
            cs = slice(qb * 512, (qb + 1) * 512)
            banks = (0, 1) if cnt5['s'] % 2 == 0 else (2, 3)
            cnt5['s'] += 1
            pis = (cnt5['pt'] % 4, (cnt5['pt'] + 1) % 4)
            cnt5['pt'] += 2
            for t in range(2):
                kt = kp * 2 + t
                ks = slice(kt * 128, (kt + 1) * 128)
                P.op('pe', (lambda t, ks: lambda e: e.matmul(out=ps[banks[t]][:, :], lhsT=knT[:, ks], rhs=qnT[:, cs],
                                                            start=True, stop=False))(t, ks),
                     reads=[B_kn[kt // 4], B_qn[qb]], writes=[psb[banks[t]]])
            for t in range(2):
                kt = kp * 2 + t
                ks = slice(kt * 128, (kt + 1) * 128)
                lo = t * 64
                P.op('pe', (lambda t, ks, lo: lambda e: e.matmul(out=ps[banks[t]][:, :], lhsT=kpeT[lo:lo + 64, ks], rhs=qpeT[lo:lo + 64, cs],
                                                                start=False, stop=True))(t, ks, lo),
                     reads=[B_kpe[kt // 4], B_qpe[qb]], writes=[psb[banks[t]]])
            for t in range(2):
                P.op('act', (lambda t: lambda e: e.activation(out=PT5[pis[t]], in_=ps[banks[t]][:, :], func=AF.Exp, scale=SC_B))(t),
                     reads=[psb[banks[t]]], writes=[B_PT5[pis[t]]])
            return pis

        def emit_PV(u, pis, h=h):
            qb, kp = u
            cs = slice(qb * 512, (qb + 1) * 512)
            nb, db = 4 + qb % 2, 6 + qb % 2
            for t in range(2):
                kt = kp * 2 + t
                pi = pis[t]
                P.op('pe', (lambda kt, pi: lambda e: e.matmul(out=ps[nb][:, :], lhsT=vtok[:, kt, :], rhs=PT5[pi],
                                                             start=(kt == 0), stop=(kt == 31)))(kt, pi),
                     reads=[B_vt[kt // 4], B_PT5[pi]], writes=[psb[nb]])
                ad, bad = accD[qb % 2], B_accD[qb % 2]
                if t == 0 and kp % 2 == 0:
                    P.op('pe', (lambda kt, pi: lambda e: e.matmul(out=ps[db][:, :], lhsT=ones, rhs=PT5[pi],
                                                                 start=(kt == 0), stop=False))(kt, pi),
                         reads=[B_const, B_PT5[pi]], writes=[psb[db]])
                elif kt == 1:
                    P.op('dve', (lambda pi: lambda e: e.tensor_copy(out=ad, in_=PT5[pi]))(pi), reads=[B_PT5[pi]], writes=[bad])
                else:
                    P.op('dve', (lambda pi: lambda e: e.tensor_tensor(out=ad, in0=ad, in1=PT5[pi], op=ALU.add))(pi),
                         reads=[B_PT5[pi], bad], writes=[bad])
            if kp == 15:
                ad, bad = accD[qb % 2], B_accD[qb % 2]
                P.op('pe', lambda e: e.matmul(out=ps[db][:, :], lhsT=ones32, rhs=ad, start=False, stop=True),
                     reads=[B_const, bad], writes=[psb[db]])
                P.op('act', lambda e: e.activation(out=rec5, in_=ps[db][:, :], func=AF.Ln), reads=[psb[db]], writes=[B_rec5])
                P.op('act', lambda e: e.activation(out=rec5, in_=rec5, func=AF.Exp, scale=-1.0), reads=[B_rec5], writes=[B_rec5])
                P.op('dve', lambda e: e.tensor_tensor(out=obT[:, h, cs], in0=ps[nb][:, :], in1=rec5, op=ALU.mult),
                     reads=[psb[nb], B_rec5], writes=[B_obT[h][qb]])

        prev = None
        for u in units:
            pis = emit_S(u)
            if prev is not None:
                emit_PV(*prev)
            prev = (u, pis)
        emit_PV(*prev)
    P.barrier()

    if stage in (3, 31):
        stg = AR.view(LOC, [128, S], F32)
        B_stg = Buf("stg")
        for c in range(4):
            P.op('dve', (lambda c: lambda e: e.tensor_copy(out=stg, in_=obT[:, c, :]))(c), reads=B_obT[c], writes=[B_stg])
            P.dma('sp', dbg[:, c, :], stg, reads=[B_stg])
        P.finish()
        P.emit()
        return nc


    def mixT(c):
        return (oaT[:, c, :], B_oaT[c]) if c < 4 else (obT[:, c - 4, :], B_obT[c - 4])

    x1 = AR.view(8192, [128, 4, D], F32)
    B_x1 = [Buf("x1_%d" % i) for i in range(4)]
    wmq = AR.view(57344, [128, 8, 512], BF16)
    wmo = AR.view(65536, [128, 4, D], BF16)
    WDs = [AR.view(73728, [128, 8, D], BF16), AR.view(194560, [128, 8, D], BF16)]
    B_WDs = [Buf("WD0"), Buf("WD1")]
    B_wmq, B_wmo = Buf("wmq"), Buf("wmo")
    T0 = 122880
    gmq = AR.view(T0, [128, D], F32)
    gff = AR.view(T0 + 4096, [128, D], F32)
    gfi = AR.view(T0 + 8192, [128, D], F32)
    B_g3 = Buf("g3")
    mkkT = AR.view(135168, [128, 4, 256], BF16)
    mvv = AR.view(137216, [128, 2, 512], BF16)
    B_mkk, B_mvv = Buf("mkk"), Buf("mvv")
    xl = [AR.view(139264 + i * 4096, [128, D], F32) for i in range(2)]
    B_xl = [Buf("xl%d" % i) for i in range(2)]
    xn2 = [AR.view(147456 + i * 2048, [128, D], BF16) for i in range(2)]
    B_xn2 = [Buf("xn2_%d" % i) for i in range(2)]
    hg = AR.view(151552, [128, 8, 512], BF16)
    B_hg = [Buf("hg%d" % i) for i in range(4)]
    mqT = AR.view(159744, [128, 4, 512], BF16)
    moT = mqT
    B_mq = [Buf("mq%d" % i) for i in range(4)]
    B_mo = B_mq
    PT6 = [AR.view(167936 + i * 1024, [128, 512], BF16) for i in range(2)]
    B_PT6 = [Buf("PT6_%d" % i) for i in range(2)]
    rec6 = AR.view(169984, [128, 512], F32)
    B_rec6 = Buf("rec6")
    actT = AR.view(172032, [128, 8, 512], BF16)
    B_act = [Buf("act%d" % i) for i in range(8)]
    gtmp = [AR.view(180224 + i * 2048, [128, 512], F32) for i in range(2)]
    B_gtmp = [Buf("gtmp%d" % i) for i in range(2)]
    xn2 = xn2 + [AR.view(180224 + i * 2048, [128, D], BF16) for i in range(2)]
    B_xn2 = B_xn2 + B_gtmp
    WGO = [184320, 188416, 163840]
    wg = [AR.view(WGO[i], [128, 8, 128], BF16) for i in range(3)]
    wu = [AR.view(WGO[i] + 2048, [128, 8, 128], BF16) for i in range(3)]
    B_wg = [Buf("wg%d" % i) for i in range(3)]
    B_wu = [Buf("wu%d" % i) for i in range(3)]
    junk2 = AR.view(192512, [128, D], BF16)
    B_junk2 = Buf("junk2")
    tl = {'x': 0, 'x4': 0, 'st': 0, 'b': 0, 'pT': 0, 's': 0, 'pt': 0, 'g': 0, 'o': 0, 'w': 0, 'wd': 0}

    def rms_stats4(srcs):
        n = len(srcs)
        k = tl['st'] % 6
        tl['st'] += 1
        blk = small[:, 8 + 4 * k:8 + 4 * k + n]
        bb = B_small[8 + k]
        for i, (ap_, bufs_) in enumerate(srcs):
            col = small[:, 8 + 4 * k + i:8 + 4 * k + i + 1]
            P.op('dve', (lambda ap_, col: lambda e: e.scalar_tensor_tensor(out=junk2, in0=ap_, scalar=1.0, in1=ap_, op0=ALU.mult,
                                                                          op1=ALU.mult, accum_out=col))(ap_, col),
                 reads=bufs_, writes=[B_junk2, bb])
        P.op('act', lambda e: e.activation(out=blk, in_=blk, func=AF.Sqrt, bias=EPS, scale=1.0 / D), reads=[bb], writes=[bb])
        P.op('dve', lambda e: e.reciprocal(out=blk, in_=blk), reads=[bb], writes=[bb])
        return [small[:, 8 + 4 * k + i:8 + 4 * k + i + 1] for i in range(n)], bb

    def norm_T4(srcs, gtile, dstT, dst_bufs):
        cols, bb = rms_stats4(srcs)
        for i, (ap_, bufs_) in enumerate(srcs):
            k = tl['x'] % 2
            tl['x'] += 1
            P.op('dve', (lambda ap_, i, k: lambda e: e.scalar_tensor_tensor(out=xn2[k], in0=ap_, scalar=cols[i], in1=gtile,
                                                                           op0=ALU.mult, op1=ALU.mult))(ap_, i, k),
                 reads=list(bufs_) + [bb, B_g3], writes=[B_xn2[k]])
            pb = 4 + tl['pT'] % 2
            tl['pT'] += 1
            for c in range(8):
                P.op('pe', (lambda c, k, pb: lambda e: e.transpose(out=psT[pb][:, c, :], in_=xn2[k][:, c * 128:(c + 1) * 128],
                                                                  identity=ident))(c, k, pb),
                     reads=[B_xn2[k], B_const], writes=[psb[pb]])
            P.op('act', (lambda i, pb: lambda e: e.copy(out=dstT[:, :, i * 128:(i + 1) * 128], in_=psT[pb]))(i, pb),
                 reads=[psb[pb]], writes=[dst_bufs[i]])

    def act_stats(ap_, bufs_):
        k = tl['st'] % 6
        tl['st'] += 1
        col = small[:, 8 + 4 * k:8 + 4 * k + 1]
        bb = B_small[8 + k]
        P.op('act', lambda e: e.activation(out=junk2, in_=ap_, func=AF.Square, accum_out=col), reads=bufs_, writes=[B_junk2, bb])
        P.op('act', lambda e: e.activation(out=col, in_=col, func=AF.Sqrt, bias=EPS, scale=1.0 / D), reads=[bb], writes=[bb])
        P.op('dve', lambda e: e.reciprocal(out=col, in_=col), reads=[bb], writes=[bb])
        return col, bb

    def norm_pre(ap_, bufs_, gtile):
        col, bb = act_stats(ap_, bufs_)
        k = tl['x4'] % 4
        tl['x4'] += 1
        P.op('dve', lambda e: e.scalar_tensor_tensor(out=xn2[k], in0=ap_, scalar=col, in1=gtile, op0=ALU.mult, op1=ALU.mult),
             reads=list(bufs_) + [bb, B_g3], writes=[B_xn2[k]])
        return k

    def norm_post(k, dstT, i, dst_buf, g0):
        pb = 4 + tl['pT'] % 4
        tl['pT'] += 1
        for c in range(8):
            P.op('pe', (lambda c: lambda e: e.transpose(out=psT[pb][:, c, :], in_=xn2[k][:, c * 128:(c + 1) * 128], identity=ident))(c),
                 reads=[B_xn2[k], B_const], writes=[psb[pb]])
        P.op('act', lambda e: e.copy(out=dstT[:, :, i * 128:(i + 1) * 128], in_=psT[pb]), reads=[psb[pb]], writes=[dst_buf])

    def norm_T1(ap_, bufs_, gtile, dstT, i, dst_buf):
        cols, bb = rms_stats4([(ap_, bufs_)])
        k = tl['x'] % 2
        tl['x'] += 1
        P.op('dve', lambda e: e.scalar_tensor_tensor(out=xn2[k], in0=ap_, scalar=cols[0], in1=gtile, op0=ALU.mult, op1=ALU.mult),
             reads=list(bufs_) + [bb, B_g3], writes=[B_xn2[k]])
        pb = 4 + tl['pT'] % 2
        tl['pT'] += 1
        for c in range(8):
            P.op('pe', (lambda c: lambda e: e.transpose(out=psT[pb][:, c, :], in_=xn2[k][:, c * 128:(c + 1) * 128], identity=ident))(c),
                 reads=[B_xn2[k], B_const], writes=[psb[pb]])
        P.op('act', lambda e: e.copy(out=dstT[:, :, i * 128:(i + 1) * 128], in_=psT[pb]), reads=[psb[pb]], writes=[dst_buf])

    def nextb():
        b = tl['b'] % 4
        tl['b'] += 1
        return b

    gkvb = AR.view(172032, [128, D], F32)
    wmkv = WDs[0]
    mkT = AR.view(151552, [128, 8, 256], BF16)
    B_mkT = [Buf("mkT%d" % i) for i in range(2)]
    P.dma('sp', gkvb, bass.AP(g_memkv.tensor, 0, [[0, 128], [1, D]]), writes=[B_g3])
    P.dma('pool', wmkv, w_mkv, writes=[B_WDs[0]], sembuf=Buf("wmkv_sem"))
    for mt in range(2):
        P.dma('sp', xl[mt], mem[mt * 128:(mt + 1) * 128, :], writes=[B_xl[mt]])
    for gt, src in ((gmq, g_memq), (gff, g_ffn), (gfi, g_fin)):
        P.dma('sp', gt, bass.AP(src.tensor, 0, [[0, 128], [1, D]]), writes=[B_g3])
    P.dma('pool', wmq, w_mq, writes=[B_wmq])
    P.dma('pool', wmo, w_mo, writes=[B_wmo])
    p4def = []
    p4i = 0
    for (T_, Bt, gcol0) in ((oaT, B_oaT, 5), (obT, B_obT, 9)):
        for qb in range(8):
            cs = slice(qb * 512, (qb + 1) * 512)
            sbank = 2 + p4i % 2
            p4i += 1
            for c in range(4):
                si = rr['sq'] % 2
                rr['sq'] += 1
                P.op('act', (lambda T_, c, cs, si: lambda e: e.activation(out=sqs[si], in_=T_[:, c, cs], func=AF.Square))(T_, c, cs, si),
                     reads=[Bt[c][qb]], writes=[B_sq[si]])
                P.op('pe', (lambda si, c, sbank: lambda e: e.matmul(out=ps[sbank][:, :], lhsT=ones, rhs=sqs[si], start=(c == 0), stop=(c == 3)))(si, c, sbank),
                     reads=[B_const, B_sq[si]], writes=[psb[sbank]])
            for f_ in p4def:
                f_()
            del p4def[:]

            def tail4(T_=T_, Bt=Bt, gcol0=gcol0, qb=qb, cs=cs, sbank=sbank):
                P.op('act', lambda e: e.activation(out=rstd, in_=ps[sbank][:, :], func=AF.Ln, bias=EPS, scale=1.0 / 512),
                     reads=[psb[sbank]], writes=[B_rstd])
                P.op('act', lambda e: e.activation(out=rstd, in_=rstd, func=AF.Exp, scale=-0.5), reads=[B_rstd], writes=[B_rstd])
                for c in range(4):
                    P.op('dve', (lambda c: lambda e: e.scalar_tensor_tensor(
                        out=T_[:, c, cs], in0=T_[:, c, cs], scalar=gc[:, gcol0 + c:gcol0 + c + 1], in1=rstd,
                        op0=ALU.mult, op1=ALU.mult))(c),
                        reads=[Bt[c][qb], B_gc, B_rstd], writes=[Bt[c][qb]])
            p4def.append(tail4)
    for f_ in p4def:
        f_()
    norm_T4([(xl[mt], [B_xl[mt]]) for mt in range(2)], gkvb, mkT, B_mkT)
    for h in range(4):
        b = nextb()
        for c in range(8):
            P.op('pe', (lambda b, c, h: lambda e: e.matmul(out=ps[b][:, 0:256], lhsT=wmkv[:, c, h * 128:(h + 1) * 128], rhs=mkT[:, c, :],
                                                          start=(c == 0), stop=(c == 7)))(b, c, h),
                 reads=[B_WDs[0]] + B_mkT, writes=[psb[b]])
        P.op('dve', (lambda b, h: lambda e: e.tensor_copy(out=mkkT[:, h, :], in_=ps[b][:, 0:256]))(b, h), reads=[psb[b]], writes=[B_mkk])
    for mt in range(2):
        b = nextb()
        for c in range(8):
            P.op('pe', (lambda b, c, mt: lambda e: e.matmul(out=ps[b][:, :], lhsT=mkT[:, c, mt * 128:(mt + 1) * 128], rhs=wmkv[:, c, 512:1024],
                                                           start=(c == 0), stop=(c == 7)))(b, c, mt),
                 reads=[B_WDs[0]] + B_mkT, writes=[psb[b]])
        P.op('dve', (lambda b, mt: lambda e: e.tensor_copy(out=mvv[:, mt, :], in_=ps[b][:, :]))(b, mt), reads=[psb[b]], writes=[B_mvv])
    P.barrier()

    SC_M = 128.0 ** -0.5
    NG = 8 if stage != 61 else 1
    THIRDS = ((0, 8), (8, 8), (16, 6))

    def next_wd():
        k = tl['wd'] % 2
        tl['wd'] += 1
        return k

    pre = {}

    def prefetch_group(g):
        wk = next_wd()
        P.dma('sp', WDs[wk], wo_bf, reads=[B_wobf], writes=[B_WDs[wk]])
        xs = []
        for tt in range(2):
            k = tl['o'] % 2
            tl['o'] += 1
            P.dma('sp', xl[k], x[(g * 4 + tt) * 128:(g * 4 + tt + 1) * 128, :], writes=[B_xl[k]])
            xs.append(k)
        pre[g] = (wk, xs)

    prefetch_group(0)
    for g in range(NG):
        wk, xs = pre[g]
        WDv, B_WD = WDs[wk], B_WDs[wk]
        nslots = []
        for tt in range(4):
            T = g * 4 + tt
            if tt < 2:
                k = xs[tt]
            else:
                k = tl['o'] % 2
                tl['o'] += 1
                P.dma('sp', xl[k], x[T * 128:(T + 1) * 128, :], writes=[B_xl[k]])
            for half in range(2):
                hc = slice(half * 512, (half + 1) * 512)
                b = nextb()
                for c in range(8):
                    mt_, mb_ = mixT(c)
                    P.op('pe', (lambda b, c, mt_, T, hc, WDv: lambda e: e.matmul(out=ps[b][:, :], lhsT=mt_[:, T * 128:(T + 1) * 128],
                                                                                rhs=WDv[:, c, hc], start=(c == 0), stop=(c == 7)))(b, c, mt_, T, hc, WDv),
                         reads=[mb_[g], B_WD], writes=[psb[b]])
                P.op('dve', (lambda b, tt, hc, k: lambda e: e.tensor_tensor(out=x1[:, tt, hc], in0=ps[b][:, :], in1=xl[k][:, hc], op=ALU.add))(b, tt, hc, k),
                     reads=[psb[b], B_xl[k]], writes=[B_x1[tt]])
            nslots.append(norm_pre(x1[:, tt, :], [B_x1[tt]], gmq))
        for tt in range(4):
            norm_post(nslots[tt], hg, tt, B_hg[tt], 0)
        for h in range(4):
            b = nextb()
            for c in range(8):
                P.op('pe', (lambda b, c, h: lambda e: e.matmul(out=ps[b][:, :], lhsT=wmq[:, c, h * 128:(h + 1) * 128], rhs=hg[:, c, :],
                                                              start=(c == 0), stop=(c == 7)))(b, c, h),
                     reads=[B_wmq] + B_hg, writes=[psb[b]])
            P.op('act', (lambda b, h: lambda e: e.copy(out=mqT[:, h, :], in_=ps[b][:, :]))(b, h), reads=[psb[b]], writes=[B_mq[h]])
        units6 = [(h, mc) for h in range(4) for mc in range(2)]

        def emit_S6(u):
            h, mc = u
            sb = nextb()
            pi = tl['pt'] % 2
            tl['pt'] += 1
            P.op('pe', lambda e: e.matmul(out=ps[sb][:, :], lhsT=mkkT[:, h, mc * 128:(mc + 1) * 128], rhs=mqT[:, h, :],
                                          start=True, stop=True),
                 reads=[B_mkk, B_mq[h]], writes=[psb[sb]])
            P.op('act', lambda e: e.activation(out=PT6[pi], in_=ps[sb][:, :], func=AF.Exp, scale=SC_M),
                 reads=[psb[sb]], writes=[B_PT6[pi]])
            return pi

        def emit_PV6(u, pi):
            h, mc = u
            nb, db = (6, 7) if h % 2 == 0 else (4, 5)
            P.op('pe', lambda e: e.matmul(out=ps[nb][:, :], lhsT=mvv[:, mc, h * 128:(h + 1) * 128], rhs=PT6[pi],
                                          start=(mc == 0), stop=(mc == 1)),
                 reads=[B_mvv, B_PT6[pi]], writes=[psb[nb]])
            P.op('pe', lambda e: e.matmul(out=ps[db][:, :], lhsT=ones, rhs=PT6[pi], start=(mc == 0), stop=(mc == 1)),
                 reads=[B_const, B_PT6[pi]], writes=[psb[db]])
            if mc == 1:
                P.op('act', lambda e: e.activation(out=rec6, in_=ps[db][:, :], func=AF.Ln), reads=[psb[db]], writes=[B_rec6])
                P.op('act', lambda e: e.activation(out=rec6, in_=rec6, func=AF.Exp, scale=-1.0), reads=[B_rec6], writes=[B_rec6])
                P.op('dve', lambda e: e.tensor_tensor(out=moT[:, h, :], in0=ps[nb][:, :], in1=rec6, op=ALU.mult),
                     reads=[psb[nb], B_rec6], writes=[B_mo[h]])

        prev6 = None
        for u in units6:
            pi = emit_S6(u)
            if prev6 is not None:
                emit_PV6(*prev6)
            prev6 = (u, pi)
        emit_PV6(*prev6)
        nslots = []
        for tt in range(4):
            for half in range(2):
                hc = slice(half * 512, (half + 1) * 512)
                b = nextb()
                for h in range(4):
                    P.op('pe', (lambda b, h, tt, hc: lambda e: e.matmul(out=ps[b][:, :], lhsT=moT[:, h, tt * 128:(tt + 1) * 128], rhs=wmo[:, h, hc],
                                                                       start=(h == 0), stop=(h == 3)))(b, h, tt, hc),
                         reads=[B_mo[h], B_wmo], writes=[psb[b]])
                P.op('dve', (lambda b, tt, hc: lambda e: e.tensor_tensor(out=x1[:, tt, hc], in0=ps[b][:, :], in1=x1[:, tt, hc], op=ALU.add))(b, tt, hc),
                     reads=[psb[b], B_x1[tt]], writes=[B_x1[tt]])
            nslots.append(norm_pre(x1[:, tt, :], [B_x1[tt]], gff))
        for tt in range(4):
            norm_post(nslots[tt], hg, tt, B_hg[tt], 8)
        for (j0, n) in THIRDS:
            wk = next_wd()
            WDv, B_WD = WDs[wk], B_WDs[wk]
            P.dma('sp', WDv[:, 0:n, :], wd_bf[:, j0:j0 + n, :], reads=[B_wdbf], writes=[B_WD])
            for jj in range(n):
                j = j0 + jj
                k = tl['w'] % 3
                tl['w'] += 1
                P.dma('sp', wg[k], wg_bf[:, j, :, :], reads=[B_wgbf], writes=[B_wg[k]])
                P.dma('sp', wu[k], wu_bf[:, j, :, :], reads=[B_wubf], writes=[B_wu[k]])
                gb_, ub_ = j % 2, 2 + j % 2
                k2 = j % 2
                for c in range(8):
                    P.op('pe', (lambda gb_, c, k: lambda e: e.matmul(out=ps[gb_][:, :], lhsT=wg[k][:, c, :], rhs=hg[:, c, :],
                                                                    start=(c == 0), stop=(c == 7)))(gb_, c, k),
                         reads=[B_wg[k]] + B_hg, writes=[psb[gb_]])
                for c in range(8):
                    P.op('pe', (lambda ub_, c, k: lambda e: e.matmul(out=ps[ub_][:, :], lhsT=wu[k][:, c, :], rhs=hg[:, c, :],
                                                                    start=(c == 0), stop=(c == 7)))(ub_, c, k),
                         reads=[B_wu[k]] + B_hg, writes=[psb[ub_]])
                P.op('act', (lambda gb_, k2: lambda e: e.activation(out=gtmp[k2], in_=ps[gb_][:, :], func=AF.Silu))(gb_, k2),
                     reads=[psb[gb_]], writes=[B_gtmp[k2]])
                P.op('dve', (lambda ub_, k2, jj: lambda e: e.tensor_tensor(out=actT[:, jj, :], in0=ps[ub_][:, :], in1=gtmp[k2], op=ALU.mult))(ub_, k2, jj),
                     reads=[psb[ub_], B_gtmp[k2]], writes=[B_act[jj]])
            last_third = (j0 + n == 22)
            if last_third and g + 1 < NG:
                prefetch_group(g + 1)
            for tt in range(4):
                for half in range(2):
                    hc = slice(half * 512, (half + 1) * 512)
                    b = 4 + (tl['g'] % 4)
                    tl['g'] += 1
                    for jj in range(n):
                        P.op('pe', (lambda b, jj, tt, hc, n, WDv: lambda e: e.matmul(out=ps[b][:, :], lhsT=actT[:, jj, tt * 128:(tt + 1) * 128],
                                                                                    rhs=WDv[:, jj, hc], start=(jj == 0), stop=(jj == n - 1)))(b, jj, tt, hc, n, WDv),
                             reads=[B_act[jj], B_WD], writes=[psb[b]])
                    P.op('dve', (lambda b, tt, hc: lambda e: e.tensor_tensor(out=x1[:, tt, hc], in0=ps[b][:, :], in1=x1[:, tt, hc], op=ALU.add))(b, tt, hc),
                         reads=[psb[b], B_x1[tt]], writes=[B_x1[tt]])
                if last_third:
                    T = g * 4 + tt
                    col, bb = act_stats(x1[:, tt, :], [B_x1[tt]])
                    cols = [col]
                    P.op('dve', (lambda tt, cols: lambda e: e.scalar_tensor_tensor(out=x1[:, tt, :], in0=x1[:, tt, :], scalar=cols[0], in1=gfi,
                                                                                  op0=ALU.mult, op1=ALU.mult))(tt, cols),
                         reads=[B_x1[tt], bb, B_g3], writes=[B_x1[tt]])
                    P.dma('sp', out[T * 128:(T + 1) * 128, :], x1[:, tt, :], reads=[B_x1[tt]])
    P.barrier()

    P.finish()
    P.emit()
    return nc


def make_consts():
    c = np.zeros((128, 1024), np.float32)
    c[:, 0:128] = np.eye(128, dtype=np.float32)
    kk = np.arange(128)[:, None]
    qq = np.arange(256)[None, :]
    valid = (qq - kk >= 0) & (qq - kk <= 128)
    c[:, 128:384] = np.where(valid, 0.0, 1.0e6)
    c[0:64, 384:448] = np.eye(64, dtype=np.float32)
    c[0:64, 512 + 64:640] = np.eye(64, dtype=np.float32)
    c[64, 640:704] = 1.0
    return c


def prep_inputs(inputs):
    f = lambda a: np.ascontiguousarray(np.asarray(a, dtype=np.float32))
    w_in = f(inputs['w_in'])[0]
    kr = w_in[:, 2176:2240]
    kr_sw = np.concatenate([kr[:, 32:64], kr[:, 0:32]], axis=1)
    w_in_ext = np.concatenate([w_in, kr, kr_sw, kr_sw], axis=1)
    wq = f(inputs['w_q_up'])[0].reshape(384, 4, 192)
    wq_sw = np.concatenate([wq[:, :, 160:192], wq[:, :, 128:160]], axis=2)
    wq_ext = np.concatenate([wq, wq[:, :, 128:192], wq_sw, wq_sw], axis=2).reshape(384, 1536)
    pm = lambda w, c: np.ascontiguousarray(w.reshape(c, 128, -1).transpose(1, 0, 2))
    pj = lambda w: np.ascontiguousarray(w.reshape(8, 128, 22, 128).transpose(1, 2, 0, 3))
    gcols = np.zeros((128, 16), np.float32)
    gcols[:, 0:3] = f(inputs['q_norm'])[0].reshape(3, 128).T
    gcols[:, 3:5] = f(inputs['kv_norm'])[0].reshape(2, 128).T
    gcols[:, 5:9] = f(inputs['gout_a'])[0].reshape(4, 128).T
    gcols[:, 9:13] = f(inputs['gout_b'])[0].reshape(4, 128).T
    half = 32
    inv_freq = (np.float32(10000.0) ** (-np.arange(half, dtype=np.float32) / np.float32(half))).astype(np.float32)
    gcols[:, 13] = np.tile(inv_freq, 4)
    gcols[:, 14] = np.tile(np.concatenate([-np.ones(32, np.float32), np.ones(32, np.float32)]), 2)
    gcols2 = np.zeros((128, 16), np.float32)
    gcols2[:, 0:8] = f(inputs['norm_mem_q'])[0].reshape(8, 128).T
    gcols2[:, 8:16] = f(inputs['norm_ffn'])[0].reshape(8, 128).T
    shared = {
        'cst': make_consts(), 'gcols': gcols, 'gcols2': gcols2,
        'g_mix': f(inputs['norm_mix']).reshape(1, D), 'g_memq': f(inputs['norm_mem_q']).reshape(1, D),
        'g_memkv': f(inputs['norm_mem_kv']).reshape(1, D), 'g_ffn': f(inputs['norm_ffn']).reshape(1, D),
        'g_fin': f(inputs['norm_final']).reshape(1, D),
        'w_in': pm(w_in_ext, 8), 'w_qup': pm(wq_ext, 3), 'w_kvup': pm(f(inputs['w_kv_up'])[0], 2),
        'w_out': pm(f(inputs['w_out'])[0], 8), 'w_mq': pm(f(inputs['w_mq'])[0], 8),
        'w_mkv': pm(f(inputs['w_mkv'])[0], 8), 'w_mo': pm(f(inputs['w_mo'])[0], 4),
        'w_gate': pj(f(inputs['w_gate'])[0]), 'w_up': pj(f(inputs['w_up'])[0]),
        'w_down': pm(f(inputs['w_down'])[0], 22),
    }
    xs = f(inputs['x'])
    mems = f(inputs['mem'])
    poss = np.ascontiguousarray(np.asarray(inputs['positions'], dtype=np.int32))
    in_maps = []
    for b in range(NCORES):
        p = poss[b]
        pk = np.zeros((128, 96), np.int32)
        for L in range(3):
            for j in range(32):
                st, step = layout_tokens(L, j)
                pk[:, L * 32 + j] = p[st:st + 128 * step:step]
        m = dict(shared)
        m['x'] = xs[b]
        m['mem'] = mems[b]
        m['pos'] = p.reshape(1, S)
        m['posk'] = pk
        in_maps.append(m)
    return in_maps


_CACHE = {}


def kernel(**inputs):
    in_maps = prep_inputs(inputs)
    if 'nc' not in _CACHE:
        _CACHE['nc'] = build()
    res = run_bass_kernel_spmd(_CACHE['nc'], in_maps, core_ids=list(range(NCORES)))
    return np.stack([np.asarray(r['out'], dtype=np.float32) for r in res.results], axis=0)
```
